# Optimizing a Trainium2 kernel written in Bass

```python
import math
import jax, jax.numpy as jnp
from jax import lax
import numpy as np

D_MODEL = 1024
BATCH = 4
SEQ = 4096
DEPTH = 2
DEC_BATCH = 128
DEC_SEQ = 4
PAST_LEN = 2048
PAGE_SIZE = 128

N_HEADS = 16
N_KV_HEADS = 4
GROUP = N_HEADS // N_KV_HEADS
HEAD_DIM = 64
Q_DIM = N_HEADS * HEAD_DIM
KV_DIM = N_KV_HEADS * HEAD_DIM
IN_COLS = Q_DIM + 6 * KV_DIM + 3 * N_HEADS
CMP_BLK = 32
SEL_BLK = 64
N_SEL = 16
WINDOW = 512
Q_CHUNK = 64
N_BUCKETS = 32
MAX_DISTANCE = 128
POOL_WINDOWS = (2, 4, 8, 16)
N_POOL_GROUPS = len(POOL_WINDOWS)
POOL_GROUP_DIM = D_MODEL // N_POOL_GROUPS
POOL_STATE = max(POOL_WINDOWS) - 1
D_FF = ((8 * D_MODEL + 3 * 256 - 1) // (3 * 256)) * 256
N_NSA_LAYERS = (DEPTH + 1) // 2
N_POOL_LAYERS = DEPTH // 2
RMS_EPS = 1e-6
NEG = -1e30
BIG = 1e9

kernel_name = 'nsa_pool_hybrid_step'


def rmsnorm(x, g):
    xf = x.astype(jnp.float32)
    y = xf * lax.rsqrt(jnp.mean(xf * xf, axis=-1, keepdims=True) + RMS_EPS)
    return (y * g.astype(jnp.float32)).astype(x.dtype)


def ada_modulate(c, w_ada, b_ada):
    m = jax.nn.silu(c) @ w_ada + b_ada
    return jnp.split(m[:, None, :], 6, axis=-1)


def rel_bucket(dist):
    d = jnp.maximum(dist, 0)
    max_exact = N_BUCKETS // 2
    large = max_exact + (jnp.log(jnp.maximum(d, 1).astype(jnp.float32) / max_exact)
                         / math.log(MAX_DISTANCE / max_exact) * (N_BUCKETS - max_exact)).astype(jnp.int32)
    large = jnp.minimum(large, N_BUCKETS - 1)
    return jnp.where(d < max_exact, d, large)


def masked_softmax(logits, mask):
    l = jnp.where(mask, logits.astype(jnp.float32), NEG)
    m = jnp.max(l, axis=-1, keepdims=True)
    p = jnp.where(mask, jnp.exp(l - m), 0.0)
    return p / jnp.maximum(jnp.sum(p, axis=-1, keepdims=True), 1e-30)


def nsa_project(h, w_in):
    b, t = h.shape[:2]
    proj = h @ w_in
    parts = jnp.split(proj, [Q_DIM + i * KV_DIM for i in range(7)], axis=-1)
    q = parts[0].reshape(b, t, N_KV_HEADS, GROUP, HEAD_DIM)
    kvs = [p.reshape(b, t, N_KV_HEADS, HEAD_DIM) for p in parts[1:7]]
    gates = parts[7].reshape(b, t, N_KV_HEADS, GROUP, 3)
    return q, kvs, gates


def compress_blocks(k_full, w_c, pe_c):
    b, L = k_full.shape[:2]
    nc = L // CMP_BLK
    blk = k_full[:, :nc * CMP_BLK].reshape(b, nc, CMP_BLK, N_KV_HEADS, HEAD_DIM) + pe_c[None, None, :, None, :]
    return jnp.einsum('bnjkd,jde->bnke', blk, w_c)


def select_blocks(k_full):
    b, L = k_full.shape[:2]
    ns = -(-L // SEL_BLK)
    kp = jnp.pad(k_full, ((0, 0), (0, ns * SEL_BLK - L), (0, 0), (0, 0)))
    return kp.reshape(b, ns, SEL_BLK, N_KV_HEADS, HEAD_DIM).transpose(0, 3, 1, 2, 4)


def nsa_attend(q, gates, qpos, k_cmp, v_cmp, k_sb, v_sb, kw, vw, kwpos, rel_bias):
    b, c = q.shape[:2]
    scale = HEAD_DIM ** -0.5
    tab = rel_bias.reshape(N_BUCKETS, N_KV_HEADS, GROUP)
    nc = k_cmp.shape[1]
    cend = jnp.arange(nc, dtype=jnp.int32) * CMP_BLK + (CMP_BLK - 1)
    dc = qpos[:, None] - cend[None, :]
    bias_c = jnp.transpose(tab[rel_bucket(dc)], (2, 3, 0, 1))
    lc = jnp.einsum('bckgd,bnkd->bkgcn', q, k_cmp) * scale + bias_c
    p_c = masked_softmax(lc, dc >= 0)
    o_c = jnp.einsum('bkgcn,bnkd->bckgd', p_c.astype(v_cmp.dtype), v_cmp)
    ns = k_sb.shape[2]
    ratio = SEL_BLK // CMP_BLK
    imp = jnp.pad(jnp.sum(p_c, axis=2), ((0, 0), (0, 0), (0, 0), (0, ratio * ns - nc)))
    imp = imp.reshape(b, N_KV_HEADS, c, ns, ratio).sum(-1)
    blk = jnp.arange(ns, dtype=jnp.int32)[None, :]
    qblk = (qpos // SEL_BLK)[:, None]
    forced = (blk == 0) | (blk == qblk) | (blk == qblk - 1)
    eligible = blk <= qblk
    imp = jnp.where(forced, BIG, jnp.where(eligible, imp, -1.0))
    _, idx = lax.top_k(imp, min(N_SEL, ns))
    gather = jax.vmap(jax.vmap(lambda kb, i: kb[i]))
    ks = gather(k_sb, idx)
    vs = gather(v_sb, idx)
    spos = idx[..., None] * SEL_BLK + jnp.arange(SEL_BLK, dtype=jnp.int32)
    ds = qpos[None, None, :, None, None] - spos
    bias_s = tab[rel_bucket(ds), jnp.arange(N_KV_HEADS)[None, :, None, None, None]]
    bias_s = jnp.moveaxis(bias_s, -1, 2)
    ls = jnp.einsum('bckgd,bkcjsd->bkgcjs', q, ks) * scale + bias_s
    ls = ls.reshape(b, N_KV_HEADS, GROUP, c, -1)
    p_s = masked_softmax(ls, (ds >= 0).reshape(b, N_KV_HEADS, 1, c, -1))
    o_s = jnp.einsum('bkgcx,bkcxd->bckgd', p_s.astype(vs.dtype), vs.reshape(b, N_KV_HEADS, c, -1, HEAD_DIM))
    dw = qpos[:, None] - kwpos[None, :]
    bias_w = jnp.transpose(tab[rel_bucket(dw)], (2, 3, 0, 1))
    lw = jnp.einsum('bckgd,blkd->bkgcl', q, kw) * scale + bias_w
    p_w = masked_softmax(lw, (dw >= 0) & (dw < WINDOW) & (kwpos[None, :] >= 0))
    o_w = jnp.einsum('bkgcl,blkd->bckgd', p_w.astype(vw.dtype), vw)
    g = jax.nn.sigmoid(gates.astype(jnp.float32))
    o = g[..., 0:1] * o_c + g[..., 1:2] * o_s + g[..., 2:3] * o_w
    return o.reshape(b, c, Q_DIM).astype(q.dtype)


def nsa_prompt(h, w_in, w_out, wk_c, wv_c, pek_c, pev_c, rel_bias):
    b, t = h.shape[:2]
    q, (kc, vc, ksl, vsl, kw, vw), gates = nsa_project(h, w_in)
    k_cmp = compress_blocks(kc, wk_c, pek_c)
    v_cmp = compress_blocks(vc, wv_c, pev_c)
    k_sb = select_blocks(ksl)
    v_sb = select_blocks(vsl)
    kw_pad = jnp.pad(kw, ((0, 0), (WINDOW, 0), (0, 0), (0, 0)))
    vw_pad = jnp.pad(vw, ((0, 0), (WINDOW, 0), (0, 0), (0, 0)))
    n_chunks = t // Q_CHUNK
    qc = jnp.moveaxis(q.reshape(b, n_chunks, Q_CHUNK, N_KV_HEADS, GROUP, HEAD_DIM), 1, 0)
    gc = jnp.moveaxis(gates.reshape(b, n_chunks, Q_CHUNK, N_KV_HEADS, GROUP, 3), 1, 0)
    starts = jnp.arange(n_chunks, dtype=jnp.int32) * Q_CHUNK

    def one_chunk(args):
        q_i, g_i, s = args
        qpos = s + jnp.arange(Q_CHUNK, dtype=jnp.int32)
        kw_i = lax.dynamic_slice_in_dim(kw_pad, s, WINDOW + Q_CHUNK, axis=1)
        vw_i = lax.dynamic_slice_in_dim(vw_pad, s, WINDOW + Q_CHUNK, axis=1)
        kwpos = s - WINDOW + jnp.arange(WINDOW + Q_CHUNK, dtype=jnp.int32)
        return nsa_attend(q_i, g_i, qpos, k_cmp, v_cmp, k_sb, v_sb, kw_i, vw_i, kwpos, rel_bias)

    o = lax.map(one_chunk, (qc, gc, starts))
    o = jnp.moveaxis(o, 0, 1).reshape(b, t, Q_DIM)
    win = min(WINDOW, t)
    return o @ w_out, (kc, vc, ksl, vsl, kw[:, t - win:], vw[:, t - win:])


def nsa_sample(h, c_kc, c_vc, c_ks, c_vs, page_table, win_k, win_v, w_in, w_out, wk_c, wv_c, pek_c, pev_c, rel_bias):
    b, t = h.shape[:2]
    past = page_table.shape[1] * PAGE_SIZE
    q, (kc, vc, ksl, vsl, kw, vw), gates = nsa_project(h, w_in)

    def paged(cache, new):
        old = cache[page_table].reshape(b, past, N_KV_HEADS, HEAD_DIM)
        return jnp.concatenate([old, new], axis=1)

    k_cmp = compress_blocks(paged(c_kc, kc), wk_c, pek_c)
    v_cmp = compress_blocks(paged(c_vc, vc), wv_c, pev_c)
    k_sb = select_blocks(paged(c_ks, ksl))
    v_sb = select_blocks(paged(c_vs, vsl))
    wbuf = win_k.shape[1]
    kw_ext = jnp.concatenate([win_k, kw], axis=1)
    vw_ext = jnp.concatenate([win_v, vw], axis=1)
    qpos = past + jnp.arange(t, dtype=jnp.int32)
    kwpos = past - wbuf + jnp.arange(wbuf + t, dtype=jnp.int32)
    o = nsa_attend(q, gates, qpos, k_cmp, v_cmp, k_sb, v_sb, kw_ext, vw_ext, kwpos, rel_bias)
    return o @ w_out, (kc, vc, ksl, vsl, kw_ext[:, t:], vw_ext[:, t:])


def pool_mix(u, prev, pos0, w_grp, layer_scale):
    b, t, _ = u.shape
    p = prev.shape[1]
    ext = jnp.concatenate([prev, u], axis=1).astype(jnp.float32)
    cs = jnp.concatenate([jnp.zeros((b, 1, D_MODEL), jnp.float32), lax.cumsum(ext, axis=1)], axis=1)
    pos = pos0 + jnp.arange(t, dtype=jnp.int32)
    outs = []
    for gi, w in enumerate(POOL_WINDOWS):
        sl = slice(gi * POOL_GROUP_DIM, (gi + 1) * POOL_GROUP_DIM)
        hi = cs[:, p + 1:p + 1 + t, sl]
        lo = cs[:, p + 1 - w:p + 1 - w + t, sl]
        cnt = jnp.minimum(pos + 1, w).astype(jnp.float32)[None, :, None]
        outs.append((hi - lo) / cnt)
    pooled = jnp.concatenate(outs, axis=-1) - ext[:, p:]
    mixed = jnp.einsum('btgc,gce->btge', pooled.reshape(b, t, N_POOL_GROUPS, POOL_GROUP_DIM),
                       w_grp.astype(jnp.float32)).reshape(b, t, D_MODEL)
    return (mixed * layer_scale).astype(u.dtype), ext[:, -POOL_STATE:].astype(u.dtype)


def swiglu(h, w_gate, w_up, w_down):
    return (jax.nn.silu(h @ w_gate) * (h @ w_up)) @ w_down


def run_trunk(x, c, mixer_apply, ada_w, ada_b, norm_g, final_g, ffn_wg, ffn_wu, ffn_wd):
    states = []
    for i in range(DEPTH):
        sh1, sc1, g1, sh2, sc2, g2 = ada_modulate(c, ada_w[i], ada_b[i])
        h = rmsnorm(x, norm_g[i, 0]) * (1 + sc1) + sh1
        y, st = mixer_apply(i, h)
        x = x + g1 * y
        h = rmsnorm(x, norm_g[i, 1]) * (1 + sc2) + sh2
        x = x + g2 * swiglu(h, ffn_wg[i], ffn_wu[i], ffn_wd[i])
        states.append(st)
    return rmsnorm(x, final_g), states


def setup_inputs(seed: int = 0) -> dict:
    key = jax.random.key(seed)
    ks = jax.random.split(key, 32)
    n_pages = PAST_LEN // PAGE_SIZE
    n_used = DEC_BATCH * n_pages
    n_phys = n_used + n_used // 4
    wbuf = min(WINDOW, PAST_LEN)
    nrm = lambda k, shape, s: jax.random.normal(k, shape, jnp.float32) * s
    page_table = jax.random.permutation(ks[0], n_phys)[:n_used].reshape(DEC_BATCH, n_pages).astype(jnp.int32)
    cache_shape = (N_NSA_LAYERS, n_phys, PAGE_SIZE, N_KV_HEADS, HEAD_DIM)
    win_shape = (N_NSA_LAYERS, DEC_BATCH, wbuf, N_KV_HEADS, HEAD_DIM)
    return {
        'x_prompt': nrm(ks[1], (BATCH, SEQ, D_MODEL), 1.0),
        'x_sample': nrm(ks[2], (DEC_BATCH, DEC_SEQ, D_MODEL), 1.0),
        'c_prompt': nrm(ks[3], (BATCH, D_MODEL), 1.0),
        'c_sample': nrm(ks[4], (DEC_BATCH, D_MODEL), 1.0),
        'cache_k_cmp': nrm(ks[5], cache_shape, 1.0),
        'cache_v_cmp': nrm(ks[6], cache_shape, 1.0),
        'cache_k_sel': nrm(ks[7], cache_shape, 1.0),
        'cache_v_sel': nrm(ks[8], cache_shape, 1.0),
        'page_table': page_table,
        'state_k_win': nrm(ks[9], win_shape, 1.0),
        'state_v_win': nrm(ks[10], win_shape, 1.0),
        'state_pool': nrm(ks[11], (N_POOL_LAYERS, DEC_BATCH, POOL_STATE, D_MODEL), 1.0),
        'rel_bias': nrm(ks[12], (N_BUCKETS, N_HEADS), 0.2),
        'ada_w': nrm(ks[13], (DEPTH, D_MODEL, 6 * D_MODEL), 0.5 * D_MODEL ** -0.5),
        'ada_b': nrm(ks[14], (DEPTH, 6 * D_MODEL), 0.01),
        'norm_g': 1.0 + nrm(ks[15], (DEPTH, 2, D_MODEL), 0.01),
        'final_g': 1.0 + nrm(ks[16], (D_MODEL,), 0.01),
        'nsa_w_in': nrm(ks[17], (N_NSA_LAYERS, D_MODEL, IN_COLS), D_MODEL ** -0.5),
        'nsa_w_out': nrm(ks[18], (N_NSA_LAYERS, Q_DIM, D_MODEL), Q_DIM ** -0.5),
        'cmp_wk': nrm(ks[19], (N_NSA_LAYERS, CMP_BLK, HEAD_DIM, HEAD_DIM), (CMP_BLK * HEAD_DIM) ** -0.5),
        'cmp_wv': nrm(ks[20], (N_NSA_LAYERS, CMP_BLK, HEAD_DIM, HEAD_DIM), (CMP_BLK * HEAD_DIM) ** -0.5),
        'cmp_pe_k': nrm(ks[21], (N_NSA_LAYERS, CMP_BLK, HEAD_DIM), 0.02),
        'cmp_pe_v': nrm(ks[22], (N_NSA_LAYERS, CMP_BLK, HEAD_DIM), 0.02),
        'pool_w': nrm(ks[23], (N_POOL_LAYERS, N_POOL_GROUPS, POOL_GROUP_DIM, POOL_GROUP_DIM), POOL_GROUP_DIM ** -0.5),
        'pool_scale': 1.0 + nrm(ks[24], (N_POOL_LAYERS, D_MODEL), 0.1),
        'ffn_wg': nrm(ks[25], (DEPTH, D_MODEL, D_FF), D_MODEL ** -0.5),
        'ffn_wu': nrm(ks[26], (DEPTH, D_MODEL, D_FF), D_MODEL ** -0.5),
        'ffn_wd': nrm(ks[27], (DEPTH, D_FF, D_MODEL), D_FF ** -0.5),
    }


def reference(x_prompt, x_sample, c_prompt, c_sample, cache_k_cmp, cache_v_cmp, cache_k_sel, cache_v_sel,
              page_table, state_k_win, state_v_win, state_pool, rel_bias, ada_w, ada_b, norm_g, final_g,
              nsa_w_in, nsa_w_out, cmp_wk, cmp_wv, cmp_pe_k, cmp_pe_v, pool_w, pool_scale,
              ffn_wg, ffn_wu, ffn_wd):
    past = page_table.shape[1] * PAGE_SIZE

    def prompt_mixer(i, h):
        j = i // 2
        if i % 2 == 0:
            return nsa_prompt(h, nsa_w_in[j], nsa_w_out[j], cmp_wk[j], cmp_wv[j], cmp_pe_k[j], cmp_pe_v[j], rel_bias)
        prev = jnp.zeros((h.shape[0], POOL_STATE, D_MODEL), h.dtype)
        return pool_mix(h, prev, 0, pool_w[j], pool_scale[j])

    def sample_mixer(i, h):
        j = i // 2
        if i % 2 == 0:
            return nsa_sample(h, cache_k_cmp[j], cache_v_cmp[j], cache_k_sel[j], cache_v_sel[j], page_table,
                              state_k_win[j], state_v_win[j], nsa_w_in[j], nsa_w_out[j],
                              cmp_wk[j], cmp_wv[j], cmp_pe_k[j], cmp_pe_v[j], rel_bias)
        return pool_mix(h, state_pool[j], past, pool_w[j], pool_scale[j])

    y_prompt, st_p = run_trunk(x_prompt, c_prompt, prompt_mixer, ada_w, ada_b, norm_g, final_g, ffn_wg, ffn_wu, ffn_wd)
    y_sample, st_s = run_trunk(x_sample, c_sample, sample_mixer, ada_w, ada_b, norm_g, final_g, ffn_wg, ffn_wu, ffn_wd)

    nsa_p = [st_p[i] for i in range(0, DEPTH, 2)]
    nsa_s = [st_s[i] for i in range(0, DEPTH, 2)]
    new_pool_p = jnp.stack([st_p[i] for i in range(1, DEPTH, 2)])
    new_pool_s = jnp.stack([st_s[i] for i in range(1, DEPTH, 2)])
    new_k_cmp_p = jnp.stack([s[0] for s in nsa_p])
    new_v_cmp_p = jnp.stack([s[1] for s in nsa_p])
    new_k_sel_p = jnp.stack([s[2] for s in nsa_p])
    new_v_sel_p = jnp.stack([s[3] for s in nsa_p])
    new_k_win_p = jnp.stack([s[4] for s in nsa_p])
    new_v_win_p = jnp.stack([s[5] for s in nsa_p])
    new_k_cmp_s = jnp.stack([s[0] for s in nsa_s])
    new_v_cmp_s = jnp.stack([s[1] for s in nsa_s])
    new_k_sel_s = jnp.stack([s[2] for s in nsa_s])
    new_v_sel_s = jnp.stack([s[3] for s in nsa_s])
    new_k_win_s = jnp.stack([s[4] for s in nsa_s])
    new_v_win_s = jnp.stack([s[5] for s in nsa_s])
    return (y_prompt, y_sample,
            new_k_cmp_p, new_v_cmp_p, new_k_sel_p, new_v_sel_p, new_k_win_p, new_v_win_p, new_pool_p,
            new_k_cmp_s, new_v_cmp_s, new_k_sel_s, new_v_sel_s, new_k_win_s, new_v_win_s, new_pool_s)
```

```python
import math
from contextlib import ExitStack
import numpy as np
import concourse.bass as bass
import concourse.mybir as mybir
from concourse.bass_utils import run_bass_kernel_spmd

F32 = mybir.dt.float32
BF16 = mybir.dt.bfloat16
I32 = mybir.dt.int32
AF = mybir.ActivationFunctionType
ALU = mybir.AluOpType
AX = mybir.AxisListType

D = 1024
NHEAD = 16
INC = 2608
DFF = 2816
NEGM = -30000.0
HALO = 32
NOWN = 2048
NSAMP = 64
NCOL = HALO + NOWN + NSAMP
C_OWN = HALO
C_SMP = HALO + NOWN
UEXT = HALO + NOWN + 16 * 19


class Res:
    __slots__ = ("name", "w", "r", "sem", "cnt")

    def __init__(self, name):
        self.name = name
        self.w = None
        self.r = {}
        self.sem = None
        self.cnt = 0


class _Rec:
    def __init__(self):
        self.call = None

    def __getattr__(self, name):
        def f(*a, **k):
            assert self.call is None
            self.call = (name, a, k)
            return self
        return f


def _record(fn):
    r = _Rec()
    fn(r)
    assert r.call is not None
    name, a, k = r.call
    return lambda eng: getattr(eng, name)(*a, **k)


class Sched:
    ENGS = ["tensor", "vector", "scalar", "gpsimd", "sync"]

    def __init__(self, nc, stack, same_engine_sync=True):
        self.nc = nc
        self.stack = stack
        self.same = same_engine_sync
        self.E = {}
        for n in self.ENGS:
            self.E[n] = dict(sem=stack.enter_context(nc.semaphore("sem_" + n)), cnt=0, ops=[], seen={})
        self.dma_res = []

    def _waits(self, eng, reads, writes):
        E = self.E[eng]
        waits = {}

        def need(dep):
            if dep is None:
                return
            s, v = dep
            if waits.get(s, 0) < v:
                waits[s] = v
        for r in reads:
            need(r.w)
        for w in writes:
            need(w.w)
            for s, v in w.r.items():
                need((s, v))
        wl = []
        for s, v in waits.items():
            if s is E["sem"] and (eng == "tensor" or not self.same):
                continue
            if E["seen"].get(s, 0) >= v:
                continue
            E["seen"][s] = v
            wl.append((s, v))
        return wl

    def op(self, eng, fn, reads=(), writes=()):
        E = self.E[eng]
        wl = self._waits(eng, reads, writes)
        E["cnt"] += 1
        me = (E["sem"], E["cnt"])
        E["ops"].append((wl, _record(fn), E["sem"], 1))
        for r in reads:
            if r.r.get(me[0], 0) < me[1]:
                r.r[me[0]] = me[1]
        for w in writes:
            w.w = me
            w.r = {}

    def dma(self, q, fn, reads=(), writes=(), sem_res=None):
        E = self.E[q]
        if sem_res is None:
            sem_res = writes[0] if writes else reads[0]
        sw = (q == "gpsimd")
        if sem_res.sem is None:
            sem_res.sem = {}
            sem_res.cnt = {}
        if sw not in sem_res.sem:
            sem_res.sem[sw] = self.stack.enter_context(self.nc.semaphore("dsem%d_%s_%d" % (int(sw), sem_res.name, len(self.dma_res))))
            sem_res.cnt[sw] = 0
            self.dma_res.append((sem_res, sw))
        wl = self._waits(q, reads, writes)
        sem_res.cnt[sw] += 16
        me = (sem_res.sem[sw], sem_res.cnt[sw])
        E["ops"].append((wl, _record(fn), sem_res.sem[sw], 16))
        for r in reads:
            if r.r.get(me[0], 0) < me[1]:
                r.r[me[0]] = me[1]
        for w in writes:
            w.w = me
            w.r = {}

    def barrier(self):
        allw = [(r.sem[sw], r.cnt[sw]) for (r, sw) in self.dma_res]
        allw += [(self.E[n]["sem"], self.E[n]["cnt"]) for n in self.ENGS if self.E[n]["cnt"] > 0]
        for n in self.ENGS:
            E = self.E[n]
            wl = []
            for s, v in allw:
                if s is E["sem"]:
                    continue
                if E["seen"].get(s, 0) >= v:
                    continue
                E["seen"][s] = v
                wl.append((s, v))
            if wl:
                E["ops"].append((wl, None, None, 0))

    def emit(self, block):
        decs = dict(tensor=block.tensor, vector=block.vector, scalar=block.scalar,
                    gpsimd=block.gpsimd, sync=block.sync)
        for n in self.ENGS:
            ops = self.E[n]["ops"]

            def body(eng, ops=ops):
                for wl, fn, sem, inc in ops:
                    for s, v in wl:
                        eng.wait_ge(s, v)
                    if fn is not None:
                        fn(eng).then_inc(sem, inc)
            decs[n](body)


def rel_bucket_np(d):
    d = np.maximum(np.asarray(d, dtype=np.int64), 0)
    x = np.maximum(d, 1).astype(np.float32) / np.float32(16)
    lg = np.log(x).astype(np.float32) / np.float32(math.log(128 / 16)) * np.float32(16)
    large = 16 + lg.astype(np.int32)
    large = np.minimum(large, 31)
    return np.where(d < 16, d, large).astype(np.int64)


def fv(ap, free):
    return bass.AP(ap.tensor, ap.offset, [list(ap.ap[0])] + [list(x) for x in free])


_DBG_ALLOC = False


def build_nc(stop_after=99):
    nc = bass.Bass("TRN2", target_bir_lowering=False)

    def din(name, shape, dt=F32):
        return nc.dram_tensor(name, list(shape), dt, kind="ExternalInput").ap()

    def dout(name, shape):
        return nc.dram_tensor(name, list(shape), F32, kind="ExternalOutput").ap()

    x_ctx = din("x_ctx", [2048, D])
    x_own = din("x_own", [2048, D])
    x_s = din("x_s", [64, D])
    c_all = din("c_all", [17, D])
    vecs = din("vecs", [18, D])
    caches = [din("cache%d" % i, [2560 * 128 if stop_after >= 2 else 128, 256]) for i in range(4)]
    ptab = din("ptab", [1, 256], I32)
    st_kw = din("st_kw", [16, 512, 256])
    st_vw = din("st_vw", [16, 512, 256])
    st_pool = din("st_pool", [16, 15, D])
    rel_bias = din("rel_bias", [32, 16])
    ada_w = din("ada_w", [2, D, 6 * D])
    w_in = din("w_in", [D, INC])
    w_out = din("w_out", [D, D])
    cmp_w = din("cmp_w", [2, 32, 64, 64])
    cmp_pe = din("cmp_pe", [2, 32, 64])
    pool_w = din("pool_w", [4, 256, 256])
    ffn_wg = din("ffn_wg", [2, D, DFF])
    ffn_wu = din("ffn_wu", [2, D, DFF])
    ffn_wd = din("ffn_wd", [2, DFF, D])
    k_ident = din("k_ident", [128, 128])
    k_anti = din("k_anti", [128, 128])
    k_emat = din("k_emat", [64, 4096])
    k_ac = din("k_ac", [128, 128])
    k_ohv = din("k_ohv", [33, 384])
    k_ohg = din("k_ohg", [33, 8 * 128])
    k_sel = din("k_sel", [3, 128, 18 * 64])
    k_vt = din("k_vt", [1, 40])
    k_vc = din("k_vc", [2, 128])
    k_vcc = din("k_vcc", [128, 4])
    k_rc = din("k_rc", [4, UEXT])

    y_own = dout("y_own", [2048, D])
    y_s = dout("y_s", [64, D])
    kvraw_p = dout("kvraw_p", [4, 2048, 256])
    kvwin_p = dout("kvwin_p", [2, 512, 256])
    pool_p = dout("pool_p", [15, D])
    kvraw_s = dout("kvraw_s", [4, 64, 256])
    kvwin_s = dout("kvwin_s", [2, 16, 512, 256])
    pool_s = dout("pool_s", [16, 15, D])

    scr_v = nc.dram_tensor("scr_v", [16, 384], F32, kind="Internal").ap()
    scr_ps = nc.dram_tensor("scr_ps", [64, 1584], F32, kind="Internal").ap()

    with ExitStack() as st:
        S = Sched(nc, st)
        op, dma = S.op, S.dma

        def T(stack, name, shape, dt):
            t = stack.enter_context(nc.sbuf_tensor(name, list(shape), dt))
            if _DBG_ALLOC:
                try:
                    with nc.sbuf_tensor("dbg_probe", [128, 1 << 20], F32):
                        pass
                except AssertionError as ex:
                    print("ALLOC", name, shape, str(ex).split("have")[1][:60])
            return t

        PS = [st.enter_context(nc.psum_tensor("ps%d" % i, [128, 512], F32)) for i in range(8)]
        RPS = [Res("ps%d" % i) for i in range(8)]
        gen_rr = [0]

        def gbank():
            i = 4 + (gen_rr[0] % 4)
            gen_rr[0] += 1
            return PS[i], RPS[i]

        ident = T(st, "ident", [128, 128], F32)
        ones_bf = T(st, "ones_bf", [128, 128], BF16)
        modT = [T(st, "modT%d" % l, [128, 48, 17], F32) for l in range(2)]
        Amod = [[T(st, "Amod%d%d" % (l, s), [128, 8, 17], F32) for s in range(2)] for l in range(2)]
        vecT = T(st, "vecT", [128, 8, 18], F32)
        vcc = T(st, "vcc", [128, 4], F32)
        R_const = Res("const")
        R_mod = Res("mod")

        dma("sync", lambda e: e.dma_start(out=ident[:], in_=k_ident), writes=[R_const])
        dma("sync", lambda e: e.dma_start(out=vcc[:], in_=k_vcc), writes=[R_const])
        op("vector", lambda e: e.memset(ones_bf[:], 1.0), writes=[R_const])

        def transpose_to(dst_fn, src_ap, rsrc, npart_in, nfree_in, wdst):
            pb, rb = gbank()
            op("tensor", lambda e: e.transpose(out=pb[0:nfree_in, 0:npart_in], in_=src_ap, identity=ident[0:npart_in, 0:npart_in]),
               reads=[rsrc, R_const], writes=[rb])
            dst_fn(pb[0:nfree_in, 0:npart_in], rb)

        with ExitStack() as p0:
            c_sb = T(p0, "c_sb", [17, D], F32)
            cs_sb = T(p0, "cs_sb", [17, D], F32)
            v_sb = T(p0, "v_sb", [18, D], F32)
            cT = T(p0, "cT", [128, 8, 17], BF16)
            wada = [T(p0, "wada%d" % i, [128, 8, 512], BF16) for i in range(2)]
            R_c, R_cs, R_v, R_cT = Res("c"), Res("cs"), Res("v"), Res("cT")
            R_wada = [Res("wada0"), Res("wada1")]
            dma("sync", lambda e: e.dma_start(out=c_sb[:], in_=c_all), writes=[R_c])
            dma("sync", lambda e: e.dma_start(out=v_sb[:], in_=vecs), writes=[R_v])
            op("scalar", lambda e: e.activation(out=cs_sb[:], in_=c_sb[:], func=AF.Silu), reads=[R_c], writes=[R_cs])
            for c in range(8):
                def d1(ps, rb, c=c):
                    op("vector", lambda e: e.tensor_copy(out=cT[:, c, :], in_=ps), reads=[rb], writes=[R_cT])
                transpose_to(d1, cs_sb[:, c * 128:(c + 1) * 128], R_cs, 17, 128, None)

                def d2(ps, rb, c=c):
                    op("vector", lambda e: e.tensor_copy(out=vecT[:, c, :], in_=ps), reads=[rb], writes=[R_mod])
                transpose_to(d2, v_sb[:, c * 128:(c + 1) * 128], R_v, 18, 128, None)
            k = 0
            for l in range(2):
                for n4 in range(12):
                    wb, rw = wada[k % 2], R_wada[k % 2]
                    k += 1
                    src = ada_w[l, :, n4 * 512:(n4 + 1) * 512].rearrange("(kc p) n -> p kc n", p=128)
                    dma("gpsimd", lambda e, wb=wb, src=src: e.dma_start(out=wb[:], in_=src), writes=[rw])
                    pb, rb = gbank()
                    for mm in range(4):
                        for kc in range(8):
                            op("tensor", lambda e, wb=wb, mm=mm, kc=kc, pb=pb: e.matmul(
                                pb[:, mm * 32:mm * 32 + 17], lhsT=wb[:, kc, mm * 128:(mm + 1) * 128], rhs=cT[:, kc, :],
                                start=(kc == 0), stop=(kc == 7)), reads=[rw, R_cT], writes=[rb])
                    for mm in range(4):
                        m = n4 * 4 + mm
                        kind, c = m // 8, m % 8
                        op("vector", lambda e, l=l, m=m, mm=mm, kind=kind, c=c, pb=pb: e.tensor_scalar(
                            out=modT[l][:, m, :], in0=pb[:, mm * 32:mm * 32 + 17], scalar1=vecT[:, c, 6 + l * 6 + kind:7 + l * 6 + kind],
                            scalar2=None, op0=ALU.add), reads=[rb, R_mod], writes=[R_mod])
            for l in range(2):
                for s in range(2):
                    sc = modT[l][:, (3 * s + 1) * 8:(3 * s + 2) * 8, :]
                    g = vecT[:, :, 2 * l + s:2 * l + s + 1].to_broadcast([128, 8, 17])
                    op("vector", lambda e, l=l, s=s, sc=sc: e.tensor_scalar(out=Amod[l][s][:], in0=sc, scalar1=1.0, scalar2=None, op0=ALU.add),
                       reads=[R_mod], writes=[R_mod])
                    op("vector", lambda e, l=l, s=s, g=g: e.tensor_tensor(out=Amod[l][s][:], in0=Amod[l][s][:], in1=g, op=ALU.mult),
                       reads=[R_mod], writes=[R_mod])
            S.barrier()

        def mod_sh(l, s):
            return modT[l][:, (3 * s) * 8:(3 * s + 1) * 8, :]

        def mod_gate(l, s):
            return modT[l][:, (3 * s + 2) * 8:(3 * s + 3) * 8, :]

        def norm_mod(stk_tiles, xap, rx, N, l, s, sample, hout, rh, gain=None):
            sq, rstd, tmp, R_sq, R_rstd, R_tmp = stk_tiles
            op("scalar", lambda e: e.activation(out=sq[:, :, 0:N], in_=xap, func=AF.Square), reads=[rx], writes=[R_sq])
            pb, rb = gbank()
            for c in range(8):
                op("tensor", lambda e, c=c: e.matmul(pb[:, 0:N], lhsT=ones_bf[:], rhs=sq[:, c, 0:N], start=(c == 0), stop=(c == 7)),
                   reads=[R_sq, R_const], writes=[rb])
            op("scalar", lambda e: e.activation(out=rstd[:, 0:N], in_=pb[:, 0:N], func=AF.Sqrt, bias=epsb[:, 0:1], scale=1.0 / D),
               reads=[rb, R_const], writes=[R_rstd])
            op("vector", lambda e: e.reciprocal(out=rstd[:, 0:N], in_=rstd[:, 0:N]), reads=[R_rstd], writes=[R_rstd])
            rb_b = fv(rstd[:, 0:N], [[0, 8], [1, N]])
            op("vector", lambda e: e.tensor_tensor(out=tmp[:, :, 0:N], in0=xap, in1=rb_b, op=ALU.mult), reads=[rx, R_rstd], writes=[R_tmp])
            if not sample:
                for c in range(8):
                    op("scalar", lambda e, c=c: e.activation(out=hout[:, c, :], in_=tmp[:, c, 0:N], func=AF.Identity,
                                                             bias=mod_sh(l, s)[:, c, 0:1], scale=Amod[l][s][:, c, 0:1]),
                       reads=[R_tmp, R_mod], writes=[rh])
            else:
                A4 = fv(Amod[l][s][:, :, 1:17], [[17, 8], [1, 16], [0, 4]])
                sh4 = fv(mod_sh(l, s)[:, :, 1:17], [[17, 8], [1, 16], [0, 4]])
                t4 = tmp[:, :, 0:64].rearrange("p c (s q) -> p c s q", q=4)
                op("vector", lambda e: e.tensor_tensor(out=t4, in0=t4, in1=A4, op=ALU.mult), reads=[R_tmp, R_mod], writes=[R_tmp])
                h4 = hout.rearrange("p c (s q) -> p c s q", q=4)
                op("vector", lambda e: e.tensor_tensor(out=h4, in0=t4, in1=sh4, op=ALU.add), reads=[R_tmp, R_mod], writes=[rh])

        epsb = T(st, "epsb", [128, 1], F32)
        op("vector", lambda e: e.memset(epsb[:], 1e-6), writes=[R_const])

        oT_all = T(st, "oT_all", [128, 8, NCOL], BF16)
        R_oT = Res("oT")

        if stop_after >= 0.4:
          with ExitStack() as p1:
            win_bf = T(p1, "win_bf", [128, 8, INC], BF16)
            R_win = Res("win")
            for kc in range(8):
                dma("gpsimd", lambda e, kc=kc: e.dma_start(out=win_bf[:, kc, :], in_=w_in[kc * 128:(kc + 1) * 128, :]), writes=[R_win])
            cw = T(p1, "cw", [128, 32, 64], BF16)
            R_cw = Res("cw")
            for ty in range(2):
                dma("gpsimd", lambda e, ty=ty: e.dma_start(out=cw[ty * 64:(ty + 1) * 64, :, :], in_=cmp_w[ty].rearrange("j d e -> d j e")), writes=[R_cw])
            peT = T(p1, "peT", [128, 128], F32)
            p1s = ExitStack()
            pe_sb = None
            R_pe = Res("pe")
            R_pes = Res("pes")

            KE = T(p1, "KE", [128, 2, 4096], BF16)
            VS = T(p1, "VS", [128, 33, 2, 65], BF16)
            KWE = T(p1, "KWE", [128, 2, 2688], BF16)
            VW = T(p1, "VW", [128, 21, 2, 65], BF16)
            R_KE = [Res("KE%d" % i) for i in range(33)]
            R_VS = [Res("VS%d" % i) for i in range(33)]
            R_KW = [Res("KW%d" % i) for i in range(21)]
            R_VW = [Res("VW%d" % i) for i in range(21)]
            R_E = Res("Erows")
            op("vector", lambda e: e.memset(KE[0:64, :, :], 0.0), writes=R_KE)
            op("gpsimd", lambda e: e.memset(VS[:], 0.0), writes=R_VS)
            op("vector", lambda e: e.memset(KWE[0:64, :, :], 0.0), writes=R_KW)
            op("gpsimd", lambda e: e.memset(VW[:], 0.0), writes=R_VW)
            for g in range(2):
                dma("gpsimd", lambda e, g=g: e.dma_start(out=KE[64:128, g, :], in_=k_emat), writes=[R_E])
                dma("gpsimd", lambda e, g=g: e.dma_start(out=KWE[64:128, g, :], in_=k_emat[:, 0:2688]), writes=[R_E])

            tabB = T(p1, "tabB", [128, 512], F32)


            AC = T(p1, "AC", [128, 128], F32)
            Hd = T(p1, "Hd", [128, 16, 256], BF16)
            Gf = T(p1, "Gf", [128, 16, 256], BF16)
            selT = T(p1, "selT", [128, 3, 64], F32)
            vtb = T(p1, "vtb", [128, 40], F32)
            vcrow = T(p1, "vcrow", [128, 2, 4, 128], BF16)
            tabx = T(p1s, "tabx", [33, 16], F32)
            ohv = T(p1s, "ohv", [33, 384], F32)
            ohg = T(p1s, "ohg", [33, 1024], F32)
            anti = T(p1s, "anti", [128, 128], F32)
            hrev = T(p1s, "hrev", [128, 256], F32)
            vsb = T(p1s, "vsb", [16, 384], F32)
            pe_sb = T(p1s, "pe_sb", [32, 128], F32)
            for ty in range(2):
                dma("sync", lambda e, ty=ty: e.dma_start(out=pe_sb[:, ty * 64:(ty + 1) * 64], in_=cmp_pe[ty]), writes=[R_pes])

            def dpe(ps, rb):
                for r in range(4):
                    op("vector", lambda e, r=r: e.tensor_copy(out=peT[:, r * 32:(r + 1) * 32], in_=ps), reads=[rb], writes=[R_pe])
            transpose_to(dpe, pe_sb[:, :], R_pes, 32, 128, None)


            R_tab, R_H, R_G, R_hrev, R_vsb, R_scrv = Res("tab"), Res("H"), Res("G"), Res("hrev"), Res("vsb"), Res("scrv")
            op("vector", lambda e: e.memset(tabx[32:33, :], NEGM), writes=[R_tab])
            dma("sync", lambda e: e.dma_start(out=tabx[0:32, :], in_=rel_bias), writes=[R_tab])
            dma("sync", lambda e: e.dma_start(out=tabB[:], in_=rel_bias.rearrange("a b -> (a b)").partition_broadcast(128)), writes=[R_tab])
            dma("sync", lambda e: e.dma_start(out=ohv[:], in_=k_ohv), writes=[R_tab])
            dma("sync", lambda e: e.dma_start(out=ohg[:], in_=k_ohg), writes=[R_tab])
            dma("sync", lambda e: e.dma_start(out=anti[:], in_=k_anti), writes=[R_tab])
            dma("sync", lambda e: e.dma_start(out=AC[:], in_=k_ac), writes=[R_tab])
            pb, rb = gbank()
            op("tensor", lambda e, pb=pb: e.matmul(pb[0:16, 0:384], lhsT=tabx[:, :], rhs=ohv[:, :], start=True, stop=True), reads=[R_tab], writes=[rb])
            op("vector", lambda e, pb=pb: e.tensor_copy(out=vsb[:], in_=pb[0:16, 0:384]), reads=[rb], writes=[R_vsb])
            dma("sync", lambda e: e.dma_start(out=scr_v, in_=vsb[:]), reads=[R_vsb], writes=[R_scrv])
            for h in range(16):
                src = bass.AP(scr_v.tensor, h * 384, [[1, 128], [1, 256]])
                dma("sync", lambda e, src=src: e.dma_start(out=hrev[:], in_=src), reads=[R_scrv], writes=[R_hrev])
                pb, rb = gbank()
                op("tensor", lambda e, pb=pb: e.matmul(pb[:, 0:256], lhsT=anti[:], rhs=hrev[:], start=True, stop=True), reads=[R_tab, R_hrev], writes=[rb])
                op("vector", lambda e, pb=pb, h=h: e.tensor_scalar(out=Hd[:, h, :], in0=pb[:, 0:256], scalar1=tabB[:, 31 * 16 + h:31 * 16 + h + 1],
                                                                  scalar2=None, op0=ALU.subtract), reads=[rb, R_tab], writes=[R_H])
            for h in range(16):
                op("vector", lambda e, h=h: e.tensor_copy(out=Gf[:, h, 0:123], in_=tabB[:, 31 * 16 + h:31 * 16 + h + 1].to_broadcast([128, 123])),
                   reads=[R_tab], writes=[R_G])
            op("vector", lambda e: e.memset(Gf[:, :, 131:255], NEGM), writes=[R_G])
            pb, rb = gbank()
            for c8 in range(8):
                op("tensor", lambda e, c8=c8, pb=pb: e.matmul(pb[:, c8 * 16:(c8 + 1) * 16], lhsT=ohg[:, c8 * 128:(c8 + 1) * 128], rhs=tabx[:, :],
                                                              start=True, stop=True), reads=[R_tab], writes=[rb])
            op("vector", lambda e, pb=pb: e.tensor_copy(out=Gf[:, :, 123:131], in_=pb[:, 0:128].rearrange("p (c h) -> p h c", h=16)), reads=[rb], writes=[R_G])

            R_sel = Res("sel")
            dma("sync", lambda e: e.dma_start(out=vtb[:], in_=k_vt.rearrange("a b -> (a b)").partition_broadcast(128)), writes=[R_sel])
            for i in range(2):
                for g in range(4):
                    dma("gpsimd", lambda e, i=i, g=g: e.dma_start(out=vcrow[:, i, g, :], in_=k_vc[i].partition_broadcast(128)), writes=[R_sel])

            S.barrier()
            p1s.close()
            xt = [T(p1, "xt0", [128, D], F32)] * 2
            R_xt = [Res("xt0")] * 2
            xTt = T(p1, "xTt", [128, 8, 128], F32)
            R_xTt = Res("xTt")
            hT = T(p1, "hT", [128, 8, 128], BF16)
            R_hT = Res("hT")
            nm_tiles = (T(p1, "nm_sq", [128, 8, 128], BF16), T(p1, "nm_rstd", [128, 128], F32), T(p1, "nm_tmp", [128, 8, 128], F32),
                        Res("nm_sq"), Res("nm_rstd"), Res("nm_tmp"))
            kvtm = [T(p1, "kvtm0", [128, 1584], F32)] * 2
            R_kv = [Res("kvtm0")] * 2
            qtm = T(p1, "qtm", [128, D], F32)
            R_qtm = Res("qtm")
            XT = T(p1, "XT", [128, 2, 512], BF16)
            R_XT = Res("XT")
            KcT = T(p1, "KcT", [64, 2, 128], BF16)
            VcT = T(p1, "VcT", [64, 2, 128], F32)
            Vc = T(p1, "Vc", [128, 2, 65], BF16)
            R_Kc, R_VcT, R_Vc = Res("KcT"), Res("VcT"), Res("Vc")
            op("vector", lambda e: e.memset(KcT[:], 0.0), writes=[R_Kc])
            op("vector", lambda e: e.memset(VcT[:], 0.0), writes=[R_VcT])
            op("vector", lambda e: e.memset(Vc[:], 0.0), writes=[R_Vc])
            QM = T(p1, "QM", [128, 8, 128], BF16)
            QMW = T(p1, "QMW", [128, 8, 128], BF16)
            QMs = T(p1, "QMs", [128, 8, 64], BF16)
            QMWs = T(p1, "QMWs", [128, 8, 64], BF16)
            R_QM, R_QMW = Res("QM"), Res("QMW")
            gsig = T(p1, "gsig", [128, 48], F32)
            R_gsig = Res("gsig")
            Lc = T(p1, "Lc", [128, 4, 128], F32)
            Ec = T(p1, "Ec", [128, 4, 128], F32)
            PnT = T(p1, "PnT", [128, 4, 128], BF16)
            R_Lc, R_Ec, R_PnT = Res("Lc"), Res("Ec"), Res("PnT")
            sm = T(p1, "sm", [128, 32], F32)
            R_sm = Res("sm")
            imp = T(p1, "imp", [128, 4, 64], F32)
            top8 = T(p1, "top8", [128, 16], F32)
            R_imp = Res("imp")
            PT = [T(p1, "PT%d" % i, [128, 4, 128], BF16) for i in range(2)]
            R_PT = [Res("PT0"), Res("PT1")]
            Sf = [T(p1, "Sf0", [128, 4, 128], F32)] * 2
            R_Sf = [Res("Sf0")] * 2
            oacc = T(p1, "oacc", [128, 8, 64], F32)
            R_oacc = Res("oacc")


            GP = [0]

            def load_sel(selidx, NQ):
                for i3 in range(3):
                    dma("sync", lambda e, i3=i3: e.dma_start(out=selT[0:NQ, i3, :], in_=k_sel[i3, 0:NQ, selidx * 64:(selidx + 1) * 64]), writes=[R_sel])

            def process_kv(kv, rkv, i, vcol, slot, do_cmp_cols, win=True, sel=True):
                gp = GP[0]
                jobs = []
                if do_cmp_cols is not None:
                    jobs.append((0, "kc"))
                    jobs.append((1, "vc"))
                if sel:
                    jobs.append((2, "ks"))
                if win:
                    jobs.append((4, "kw"))
                for ty, nm in jobs:
                    pb, rb = gbank()
                    for gl in range(2):
                        g = 2 * gp + gl
                        op("tensor", lambda e, pb=pb, ty=ty, g=g, gl=gl: e.transpose(out=pb[0:64, gl * 128:(gl + 1) * 128], in_=kv[:, ty * 256 + g * 64: ty * 256 + (g + 1) * 64],
                                                                                  identity=ident[:]), reads=[rkv, R_const], writes=[rb])
                    src = pb[0:64, 0:256].rearrange("p (g t) -> p g t", g=2)
                    if nm == "ks":
                        op("scalar", lambda e, src=src: e.copy(out=KE[0:64, :, i * 128:(i + 1) * 128], in_=src), reads=[rb], writes=[R_KE[i]])
                    elif nm == "kw":
                        op("scalar", lambda e, src=src: e.copy(out=KWE[0:64, :, slot * 128:(slot + 1) * 128], in_=src), reads=[rb], writes=[R_KW[slot]])
                    else:
                        po = 0 if nm == "kc" else 64
                        pe_b = fv(peT[po:po + 64, :], [[0, 2], [1, 128]])
                        op("vector", lambda e, src=src, po=po, pe_b=pe_b: e.tensor_tensor(out=XT[po:po + 64, :, do_cmp_cols:do_cmp_cols + 128], in0=src, in1=pe_b, op=ALU.add),
                           reads=[rb, R_pe], writes=[R_XT])
                c0 = gp * 128
                if sel:
                    op("vector", lambda e: e.tensor_scalar(out=VS[:, i, :, 0:64], in0=kv[:, 768 + c0:768 + c0 + 128].rearrange("p (g d) -> p g d", g=2), scalar1=vcol,
                                                           scalar2=None, op0=ALU.mult), reads=[rkv, R_sel, R_const], writes=[R_VS[i]])
                    op("vector", lambda e: e.tensor_copy(out=VS[:, i, :, 64:65], in_=fv(vcol, [[0, 2], [1, 1]])), reads=[R_sel, R_const], writes=[R_VS[i]])
                if win:
                    op("gpsimd", lambda e: e.tensor_scalar(out=VW[:, slot, :, 0:64], in0=kv[:, 1280 + c0:1280 + c0 + 128].rearrange("p (g d) -> p g d", g=2), scalar1=vcol,
                                                           scalar2=None, op0=ALU.mult), reads=[rkv, R_sel, R_const], writes=[R_VW[slot]])
                    op("gpsimd", lambda e: e.tensor_copy(out=VW[:, slot, :, 64:65], in_=fv(vcol, [[0, 2], [1, 1]])), reads=[R_sel, R_const], writes=[R_VW[slot]])

            def compress_group(G, vccol):
                for ty in range(2):
                    po = ty * 64
                    pb, rb = gbank()
                    for gl in range(2):
                        for j in range(32):
                            rhs = fv(XT[po:po + 64, gl, j:j + 1], [[32, 16]])
                            op("tensor", lambda e, pb=pb, gl=gl, j=j, po=po, rhs=rhs: e.matmul(pb[0:64, gl * 16:(gl + 1) * 16], lhsT=cw[po:po + 64, j, :], rhs=rhs,
                                                                                               start=(j == 0), stop=(j == 31)), reads=[R_XT, R_cw], writes=[rb])
                    src = pb[0:64, 0:32].rearrange("p (g n) -> p g n", g=2)
                    if ty == 0:
                        op("vector", lambda e, src=src: e.tensor_copy(out=KcT[:, :, G * 16:(G + 1) * 16], in_=src), reads=[rb], writes=[R_Kc])
                    else:
                        op("vector", lambda e, src=src: e.tensor_copy(out=VcT[:, :, G * 16:(G + 1) * 16], in_=src), reads=[rb], writes=[R_VcT])
                pb, rb = gbank()
                for gl in range(2):
                    op("tensor", lambda e, pb=pb, gl=gl: e.transpose(out=pb[:, gl * 64:(gl + 1) * 64], in_=VcT[:, gl, :], identity=ident[0:64, 0:64]),
                       reads=[R_VcT, R_const], writes=[rb])
                op("vector", lambda e, pb=pb: e.tensor_scalar(out=Vc[:, :, 0:64], in0=pb[:, 0:128].rearrange("p (g d) -> p g d", g=2), scalar1=vccol, scalar2=None,
                                                              op0=ALU.mult), reads=[rb, R_const], writes=[R_Vc])
                op("vector", lambda e: e.tensor_copy(out=Vc[:, :, 64:65], in_=fv(vccol, [[0, 2], [1, 1]])), reads=[R_const], writes=[R_Vc])

            ATT_STAGE = int(round((stop_after - 0.75) * 100)) if 0.75 <= stop_after < 0.8 else 99

            def attend(qm, qmw, NQ, tq, qoff, selidx, gs, vcidx, winslots, ocol, osub=None):
                gp = GP[0]
                q0, qn = osub if osub is not None else (0, NQ)
                op("vector", lambda e: e.tensor_scalar(out=gs, in0=gs, scalar1=1.0, scalar2=None, op0=ALU.add), reads=[R_gsig], writes=[R_gsig])
                op("vector", lambda e: e.reciprocal(out=gs, in_=gs), reads=[R_gsig], writes=[R_gsig])
                if ATT_STAGE < 1:
                    return
                goff = 127 - 4 * tq
                load_sel(selidx, NQ)
                for gl in range(2):
                    g = 2 * gp + gl
                    pS, rS = PS[gl % 2], RPS[gl % 2]
                    for hl in range(4):
                        op("tensor", lambda e, pS=pS, hl=hl, gl=gl: e.matmul(pS[0:NQ, hl * 128:(hl + 1) * 128], lhsT=qm[0:64, 4 * gl + hl, :], rhs=KcT[:, gl, :],
                                                                             start=True, stop=True), reads=[R_QM, R_Kc], writes=[rS])
                    op("vector", lambda e, pS=pS, g=g: e.tensor_tensor(out=Lc[0:NQ], in0=pS[0:NQ, :].rearrange("p (h n) -> p h n", h=4),
                                                                       in1=Gf[qoff:qoff + NQ, 4 * g:4 * g + 4, goff:goff + 128], op=ALU.add),
                       reads=[rS, R_G], writes=[R_Lc])
                    op("vector", lambda e: e.tensor_reduce(out=sm[0:NQ, 0:4], in_=Lc[0:NQ], axis=AX.X, op=ALU.max), reads=[R_Lc], writes=[R_sm])
                    op("vector", lambda e: e.tensor_scalar(out=sm[0:NQ, 4:8], in0=sm[0:NQ, 0:4], scalar1=-1.0, scalar2=None, op0=ALU.mult), reads=[R_sm], writes=[R_sm])
                    for hl in range(4):
                        op("scalar", lambda e, hl=hl: e.activation(out=Ec[0:NQ, hl, :], in_=Lc[0:NQ, hl, :], func=AF.Exp, bias=sm[0:NQ, 4 + hl:5 + hl], scale=1.0),
                           reads=[R_Lc, R_sm], writes=[R_Ec])
                    op("vector", lambda e: e.tensor_tensor(out=Ec[0:NQ], in0=Ec[0:NQ], in1=vcrow[0:NQ, vcidx], op=ALU.mult), reads=[R_Ec, R_sel], writes=[R_Ec])
                    op("vector", lambda e: e.tensor_reduce(out=sm[0:NQ, 8:12], in_=Ec[0:NQ], axis=AX.X, op=ALU.add), reads=[R_Ec], writes=[R_sm])
                    op("vector", lambda e: e.tensor_scalar(out=sm[0:NQ, 8:12], in0=sm[0:NQ, 8:12], scalar1=1e-30, scalar2=None, op0=ALU.max), reads=[R_sm], writes=[R_sm])
                    op("vector", lambda e: e.reciprocal(out=sm[0:NQ, 12:16], in_=sm[0:NQ, 8:12]), reads=[R_sm], writes=[R_sm])
                    op("vector", lambda e: e.tensor_tensor(out=Ec[0:NQ], in0=Ec[0:NQ], in1=fv(sm[0:NQ, 12:16], [[1, 4], [0, 128]]), op=ALU.mult),
                       reads=[R_Ec, R_sm], writes=[R_Ec])
                    if ATT_STAGE < 2:
                        continue
                    op("vector", lambda e: e.tensor_reduce(out=imp[0:NQ, 0, :], in_=Ec[0:NQ].rearrange("q h (j r) -> q j h r", r=2), axis=AX.XY, op=ALU.add),
                       reads=[R_Ec], writes=[R_imp])
                    op("vector", lambda e: e.tensor_tensor(out=imp[0:NQ, 0, :], in0=imp[0:NQ, 0, :], in1=selT[0:NQ, 0, :], op=ALU.mult), reads=[R_imp, R_sel], writes=[R_imp])
                    op("vector", lambda e: e.tensor_tensor(out=imp[0:NQ, 0, :], in0=imp[0:NQ, 0, :], in1=selT[0:NQ, 1, :], op=ALU.add), reads=[R_imp, R_sel], writes=[R_imp])
                    op("vector", lambda e: e.max(out=top8[0:NQ, 0:8], in_=imp[0:NQ, 0, :]), reads=[R_imp], writes=[R_imp])
                    op("vector", lambda e: e.match_replace(out=imp[0:NQ, 1, :], in_to_replace=top8[0:NQ, 0:8], in_values=imp[0:NQ, 0, :], imm_value=-1e30),
                       reads=[R_imp], writes=[R_imp])
                    op("vector", lambda e: e.max(out=top8[0:NQ, 8:16], in_=imp[0:NQ, 1, :]), reads=[R_imp], writes=[R_imp])
                    op("vector", lambda e: e.tensor_scalar(out=imp[0:NQ, 2, :], in0=imp[0:NQ, 0, :], scalar1=top8[0:NQ, 15:16], scalar2=None, op0=ALU.is_ge),
                       reads=[R_imp], writes=[R_imp])
                    op("vector", lambda e: e.tensor_tensor(out=imp[0:NQ, 2, :], in0=imp[0:NQ, 2, :], in1=selT[0:NQ, 2, :], op=ALU.mult), reads=[R_imp, R_sel], writes=[R_imp])
                    op("vector", lambda e: e.tensor_scalar(out=imp[0:NQ, 3, :], in0=imp[0:NQ, 2, :], scalar1=-NEGM, scalar2=NEGM, op0=ALU.mult, op1=ALU.add),
                       reads=[R_imp], writes=[R_imp])
                    pb, rb = gbank()
                    op("tensor", lambda e, pb=pb: e.transpose(out=pb[0:64, 0:NQ], in_=imp[0:NQ, 3, :], identity=ident[0:NQ, 0:NQ]), reads=[R_imp, R_const], writes=[rb])
                    for hl in range(4):
                        h = 4 * g + hl
                        op("vector", lambda e, pb=pb, h=h, hl=hl, gl=gl: e.tensor_scalar(out=qm[64:128, 4 * gl + hl, :], in0=pb[0:64, 0:NQ], scalar1=tabB[64:128, 31 * 16 + h:31 * 16 + h + 1],
                                                                                         scalar2=None, op0=ALU.add), reads=[rb, R_tab], writes=[R_QM])
                    if ATT_STAGE < 3:
                        continue
                    pb, rb = gbank()
                    for hl in range(4):
                        op("tensor", lambda e, pb=pb, hl=hl: e.transpose(out=pb[:, hl * 128:hl * 128 + NQ], in_=Ec[0:NQ, hl, :], identity=ident[0:NQ, 0:NQ]),
                           reads=[R_Ec, R_const], writes=[rb])
                    op("scalar", lambda e, pb=pb: e.copy(out=PnT[:, :, 0:NQ], in_=pb[:, :].rearrange("p (h q) -> p h q", h=4)[:, :, 0:NQ]), reads=[rb], writes=[R_PnT])
                    rO = RPS[2]
                    for hl in range(4):
                        op("tensor", lambda e, hl=hl, gl=gl: e.matmul(PS[2][0:NQ, hl * 65:hl * 65 + 64], lhsT=PnT[:, hl, 0:NQ], rhs=Vc[:, gl, 0:64], start=True, stop=True),
                           reads=[R_PnT, R_Vc], writes=[rO])
                    for hl in range(4):
                        op("vector", lambda e, hl=hl, gl=gl, g=g: e.tensor_scalar(out=oacc[0:NQ, 4 * gl + hl, :], in0=PS[2][0:NQ, hl * 65:hl * 65 + 64],
                                                                                  scalar1=gs[:, g * 12 + hl * 3:g * 12 + hl * 3 + 1], scalar2=None, op0=ALU.mult),
                           reads=[rO, R_gsig], writes=[R_oacc])

                    def branch(tiles, bank_o, r_o, gcol, qsrc, rqs):
                        n = len(tiles)

                        def qk(ii):
                            lhsT, rk = tiles[ii][0], tiles[ii][1]
                            b = ii % 2
                            op("tensor", lambda e, lhsT=lhsT, b=b: e.matmul(PS[b][:, 0:4 * NQ], lhsT=lhsT, rhs=qsrc[:, 4 * gl:4 * gl + 4, :], start=True, stop=True),
                               reads=[rk, rqs, R_E], writes=[RPS[b]])
                        qk(0)
                        for ii in range(n):
                            if ii + 1 < n:
                                qk(ii + 1)
                            _, _, vap, rv, mode, delta = tiles[ii]
                            b = ii % 2
                            pv3 = PS[b][:, 0:4 * NQ].rearrange("p (h q) -> p h q", h=4)
                            if mode == "far":
                                op("scalar", lambda e, b=b, pv3=pv3: e.activation(out=PT[b][:, :, 0:NQ], in_=pv3, func=AF.Exp), reads=[RPS[b]], writes=[R_PT[b]])
                            else:
                                if mode == "hd":
                                    in1 = Hd[:, 4 * g:4 * g + 4, delta + qoff:delta + qoff + NQ]
                                else:
                                    in1 = fv(AC[:, qoff:qoff + NQ], [[0, 4], [1, NQ]])
                                op("vector", lambda e, b=b, pv3=pv3, in1=in1: e.tensor_tensor(out=Sf[b][:, :, 0:NQ], in0=pv3, in1=in1, op=ALU.add),
                                   reads=[RPS[b], R_H, R_tab], writes=[R_Sf[b]])
                                op("scalar", lambda e, b=b: e.activation(out=PT[b][:, :, 0:NQ], in_=Sf[b][:, :, 0:NQ], func=AF.Exp), reads=[R_Sf[b]], writes=[R_PT[b]])
                            for hl in range(4):
                                op("tensor", lambda e, b=b, hl=hl, vap=vap, ii=ii: e.matmul(PS[bank_o][0:NQ, hl * 65:(hl + 1) * 65], lhsT=PT[b][:, hl, 0:NQ], rhs=vap,
                                                                                           start=(ii == 0 and hl == 0), stop=(ii == n - 1 and hl == 3)), reads=[R_PT[b], rv], writes=[r_o])
                        o3 = PS[bank_o][0:NQ, 0:260].rearrange("p (h d) -> p h d", d=65)
                        op("vector", lambda e, o3=o3: e.tensor_scalar(out=sm[0:NQ, 16:20], in0=fv(PS[bank_o][0:NQ, 64:65], [[65, 4]]), scalar1=1e-30, scalar2=None, op0=ALU.max),
                           reads=[r_o], writes=[R_sm])
                        op("vector", lambda e: e.reciprocal(out=sm[0:NQ, 20:24], in_=sm[0:NQ, 16:20]), reads=[R_sm], writes=[R_sm])
                        gsl = fv(gs[:, g * 12 + gcol:g * 12 + gcol + 1], [[3, 4]])
                        op("vector", lambda e, gsl=gsl: e.tensor_tensor(out=sm[0:NQ, 24:28], in0=sm[0:NQ, 20:24], in1=gsl, op=ALU.mult), reads=[R_sm, R_gsig], writes=[R_sm])
                        for hl in range(4):
                            op("vector", lambda e, hl=hl, o3=o3: e.scalar_tensor_tensor(out=oacc[0:NQ, 4 * gl + hl, :], in0=o3[:, hl, 0:64], scalar=sm[0:NQ, 24 + hl:25 + hl],
                                                                                       in1=oacc[0:NQ, 4 * gl + hl, :], op0=ALU.mult, op1=ALU.add),
                               reads=[r_o, R_sm, R_oacc], writes=[R_oacc])

                    if ATT_STAGE < 4:
                        continue
                    tiles = []
                    for kt in range(tq + 1):
                        dl = (tq - kt) * 128
                        mode = "hd" if dl <= 128 else "far"
                        tiles.append((KE[:, gl, kt * 128:(kt + 1) * 128], R_KE[kt], VS[:, kt, gl, :], R_VS[kt], mode, dl))
                    branch(tiles, 3, RPS[3], 1, qm, R_QM)
                    tiles = []
                    for slot, dl in winslots:
                        mode = "hd" if dl <= 128 else ("ac" if dl == 512 else "far")
                        tiles.append((KWE[:, gl, slot * 128:(slot + 1) * 128], R_KW[slot], VW[:, slot, gl, :], R_VW[slot], mode, dl))
                    branch(tiles, 2, RPS[2], 2, qmw, R_QMW)
                pb, rb = gbank()
                for cc in range(4):
                    op("tensor", lambda e, pb=pb, cc=cc: e.transpose(out=pb[:, cc * 128:cc * 128 + NQ], in_=oacc[0:NQ].rearrange("q h d -> q (h d)")[:, cc * 128:(cc + 1) * 128],
                                                                     identity=ident[0:NQ, 0:NQ]), reads=[R_oacc, R_const], writes=[rb])
                op("scalar", lambda e, pb=pb, gp=gp: e.copy(out=oT_all[:, gp * 4:gp * 4 + 4, ocol:ocol + qn],
                                                            in_=pb[:, :].rearrange("p (c q) -> p c q", c=4)[:, :, q0:q0 + qn]), reads=[rb], writes=[R_oT])

            def load_q(qsrc_tm, rq, NQ, qm, qmw):
                gp = GP[0]
                for hb in range(2):
                    pb, rb = gbank()
                    for hh in range(4):
                        h = gp * 8 + hb * 4 + hh
                        op("tensor", lambda e, pb=pb, hh=hh, h=h: e.transpose(out=pb[0:64, hh * 128:hh * 128 + NQ], in_=qsrc_tm[:, h * 64:(h + 1) * 64], identity=ident[0:NQ, 0:NQ]),
                           reads=[rq, R_const], writes=[rb])
                    src = pb[0:64, :].rearrange("p (h q) -> p h q", h=4)[:, :, 0:NQ]
                    op("scalar", lambda e, src=src, hb=hb: e.activation(out=qm[0:64, hb * 4:hb * 4 + 4, :], in_=src, func=AF.Identity, scale=0.125), reads=[rb], writes=[R_QM])
                    op("vector", lambda e, hb=hb: e.tensor_copy(out=qmw[0:64, hb * 4:hb * 4 + 4, :], in_=qm[0:64, hb * 4:hb * 4 + 4, :]),
                       reads=[R_QM], writes=[R_QMW])
                for hloc in range(8):
                    if stop_after == 0.755:
                        break
                    h = gp * 8 + hloc
                    op("vector", lambda e, h=h, hloc=hloc: e.tensor_copy(out=qmw[64:128, hloc, :], in_=tabB[64:128, 31 * 16 + h:31 * 16 + h + 1].to_broadcast([64, NQ])),
                       reads=[R_tab], writes=[R_QMW])

            def project(hT_ap, rh, NT, kv, rkv, c0, c1):
                cc = c0
                while cc < c1:
                    w = min(512, c1 - cc)
                    pb, rb = gbank()
                    for kc in range(8):
                        op("tensor", lambda e, pb=pb, kc=kc, cc=cc, w=w: e.matmul(pb[0:NT, 0:w], lhsT=hT_ap[:, kc, :], rhs=win_bf[:, kc, cc:cc + w],
                                                                                 start=(kc == 0), stop=(kc == 7)), reads=[rh, R_win], writes=[rb])
                    if cc >= 1024:
                        op("scalar", lambda e, pb=pb, cc=cc, w=w: e.copy(out=kv[0:NT, cc - 1024:cc - 1024 + w], in_=pb[0:NT, 0:w]), reads=[rb], writes=[rkv])
                    else:
                        op("vector", lambda e, pb=pb, cc=cc, w=w: e.tensor_copy(out=qtm[0:NT, cc:cc + w], in_=pb[0:NT, 0:w]), reads=[rb], writes=[R_qtm])
                    cc += w

            def load_xT(src_rows, NT, xb, rxb):
                dma("sync", lambda e: e.dma_start(out=xb[0:NT, :], in_=src_rows), writes=[rxb])
                for half in range(2):
                    pb, rb = gbank()
                    for cc in range(4):
                        c = half * 4 + cc
                        op("tensor", lambda e, pb=pb, cc=cc, c=c: e.transpose(out=pb[:, cc * 128:cc * 128 + NT], in_=xb[0:NT, c * 128:(c + 1) * 128], identity=ident[0:NT, 0:NT]),
                           reads=[rxb, R_const], writes=[rb])
                    op("vector", lambda e, pb=pb, half=half: e.tensor_copy(out=xTt[:, half * 4:half * 4 + 4, 0:NT], in_=pb[:, :].rearrange("p (c q) -> p c q", c=4)[:, :, 0:NT]),
                       reads=[rb], writes=[R_xTt])

            ptb = T(p1, "ptb", [128, 256], I32)
            idx = T(p1, "idx", [128, 256], I32)
            iop = T(p1, "iop", [128, 1], F32)
            gq = T(p1, "gq", [4, 48], F32)
            R_idx, R_gq, R_scrps, R_cp = Res("idx"), Res("gq"), Res("scrps"), Res("dcopy")
            dma("sync", lambda e: e.dma_start(out=ptb[:], in_=ptab.rearrange("a b -> (a b)").partition_broadcast(128)), writes=[R_idx])
            op("gpsimd", lambda e: e.iota(iop[:], pattern=[[0, 1]], base=0, channel_multiplier=1, allow_small_or_imprecise_dtypes=True), writes=[R_idx])
            op("vector", lambda e: e.tensor_scalar(out=idx[:], in0=ptb[:], scalar1=128.0, scalar2=iop[:, 0:1], op0=ALU.mult, op1=ALU.add), reads=[R_idx], writes=[R_idx])

            for gp in range(2):
                GP[0] = gp
                if stop_after < 0.6:
                    break
                for i in range(32):
                    if (stop_after < 0.7 and i >= 4) or (stop_after < 0.8 and i >= 16):
                        break
                    xb, rxb = xt[0], R_xt[0]
                    kv, rkv = kvtm[0], R_kv[0]
                    own = i >= 16
                    src = x_own[(i - 16) * 128:(i - 15) * 128, :] if own else x_ctx[i * 128:(i + 1) * 128, :]
                    load_xT(src, 128, xb, rxb)
                    norm_mod(nm_tiles, xTt[:], R_xTt, 128, 0, 0, False, hT[:], R_hT)
                    project(hT, R_hT, 128, kv, rkv, 1024, INC if own else 2560)
                    if own and gp == 0:
                        t = i - 16
                        dma("sync", lambda e, kv=kv, t=t: e.dma_start(out=kvraw_p[:, t * 128:(t + 1) * 128, :].rearrange("t p c -> p t c"),
                                                                      in_=kv[:, 0:1024].rearrange("p (t c) -> p t c", t=4)), reads=[rkv])
                        if t >= 12:
                            dma("sync", lambda e, kv=kv, t=t: e.dma_start(out=kvwin_p[:, (t - 12) * 128:(t - 11) * 128, :].rearrange("t p c -> p t c"),
                                                                          in_=kv[:, 1024:1536].rearrange("p (t c) -> p t c", t=2)), reads=[rkv])
                    process_kv(kv, rkv, i, vtb[:, i:i + 1], max(i - 11, 0), (i % 4) * 128, win=(i >= 11))
                    if i % 4 == 3:
                        compress_group(i // 4, vcc[:, 0:1])
                    if i == 15 and stop_after >= 0.75:
                        project(hT, R_hT, 128, kv, rkv, 0, 1024)
                        project(hT, R_hT, 128, kv, rkv, 2560, INC)
                        load_q(qtm[:, :], R_qtm, 128, QM[:], QMW[:])
                        op("scalar", lambda e, kv=kv: e.activation(out=gsig[:, :], in_=kv[:, 1536:1584], func=AF.Exp, scale=-1.0), reads=[rkv], writes=[R_gsig])
                        attend(QM[:], QMW[:], 128, 15, 0, 16, gsig[:, :], 0, [(d, (4 - d) * 128) for d in range(5)], 0, osub=(96, 32))
                for i in range(16, 32):
                    if stop_after < 0.9:
                        break
                    xb, rxb = xt[0], R_xt[0]
                    kv, rkv = kvtm[0], R_kv[0]
                    t = i - 16
                    load_xT(x_own[t * 128:(t + 1) * 128, :], 128, xb, rxb)
                    norm_mod(nm_tiles, xTt[:], R_xTt, 128, 0, 0, False, hT[:], R_hT)
                    project(hT, R_hT, 128, kv, rkv, 0, 1024)
                    project(hT, R_hT, 128, kv, rkv, 2560, INC)
                    load_q(qtm[:, :], R_qtm, 128, QM[:], QMW[:])
                    op("scalar", lambda e, kv=kv: e.activation(out=gsig[:, :], in_=kv[:, 1536:1584], func=AF.Exp, scale=-1.0), reads=[rkv], writes=[R_gsig])
                    if True:
                        for d in range(5):
                            pass
                    attend(QM[:], QMW[:], 128, i, 0, t, gsig[:, :], 0, [(i - 15 + d, (4 - d) * 128) for d in range(5)], C_OWN + t * 128)

                if stop_after >= 2:
                    load_xT(x_s, 64, xt[0], R_xt[0])
                    norm_mod(nm_tiles, xTt[:, :, 0:64], R_xTt, 64, 0, 0, True, hT[:, :, 0:64], R_hT)
                    kv, rkv = kvtm[0], R_kv[0]
                    project(hT[:, :, 0:64], R_hT, 64, kv, rkv, 0, INC)
                    if gp == 0:
                        dma("sync", lambda e, kv=kv: e.dma_start(out=scr_ps, in_=kv[0:64, :]), reads=[rkv], writes=[R_scrps])
                        dma("sync", lambda e, kv=kv: e.dma_start(out=kvraw_s.rearrange("t p c -> p t c"), in_=kv[0:64, 0:1024].rearrange("p (t c) -> p t c", t=4)), reads=[rkv])
                        for s_ in range(16):
                            dma("sync", lambda e, kv=kv, s_=s_: e.dma_start(out=kvwin_s[:, s_, 508:512, :].rearrange("t q c -> q t c"),
                                                                           in_=kv[4 * s_:4 * s_ + 4, 1024:1536].rearrange("p (t c) -> p t c", t=2)), reads=[rkv])
                        for ty, stt in enumerate((st_kw, st_vw)):
                            dma("sync", lambda e, ty=ty, stt=stt: e.dma_start(out=kvwin_s[ty, :, 0:508, :], in_=stt[:, 4:512, :]), writes=[R_cp])
                    load_q(qtm[0:64, :], R_qtm, 64, QMs[:], QMWs[:])
                    for s in range(16):
                        for pg in range(17):
                            if pg < 16:
                                for ty in range(4):
                                    dma("gpsimd", lambda e, kv=kv, ty=ty, s=s, pg=pg: e.indirect_dma_start(
                                        out=kv[:, ty * 256:(ty + 1) * 256], out_offset=None, in_=caches[ty],
                                        in_offset=bass.IndirectOffsetOnAxis(ap=idx[:, s * 16 + pg:s * 16 + pg + 1], axis=0)), reads=[R_idx], writes=[rkv])
                                if pg >= 12:
                                    dma("sync", lambda e, kv=kv, s=s, pg=pg: e.dma_start(out=kv[:, 1024:1280], in_=st_kw[s, (pg - 12) * 128:(pg - 11) * 128, :]), writes=[rkv])
                                    dma("sync", lambda e, kv=kv, s=s, pg=pg: e.dma_start(out=kv[:, 1280:1536], in_=st_vw[s, (pg - 12) * 128:(pg - 11) * 128, :]), writes=[rkv])
                                process_kv(kv, rkv, pg, vcc[:, 3:4], max(pg - 11, 0), (pg % 4) * 128, win=(pg >= 12))
                                if pg % 4 == 3:
                                    compress_group(pg // 4, vcc[:, 1:2])
                            else:
                                op("vector", lambda e, kv=kv: e.memset(kv[:, 0:1536], 0.0), writes=[rkv])
                                dma("sync", lambda e, kv=kv, s=s: e.dma_start(out=kv[0:4, 0:1536], in_=scr_ps[s * 4:(s + 1) * 4, 0:1536]), reads=[R_scrps], writes=[rkv])
                                process_kv(kv, rkv, 16, vcc[:, 2:3], 5, None, win=True)
                        dma("sync", lambda e, s=s: e.dma_start(out=gq[:], in_=scr_ps[s * 4:(s + 1) * 4, 1536:1584]), reads=[R_scrps], writes=[R_gq])
                        op("scalar", lambda e: e.activation(out=gsig[0:4, :], in_=gq[:], func=AF.Exp, scale=-1.0), reads=[R_gq], writes=[R_gsig])
                        attend(QMs[:, :, s * 4:(s + 1) * 4], QMWs[:, :, s * 4:(s + 1) * 4], 4, 16, 0, 17, gsig[0:4, :], 1,
                               [(1, 512), (2, 384), (3, 256), (4, 128), (5, 0)], C_SMP + s * 4)
            S.barrier()

        if stop_after >= 3:
          with ExitStack() as p2:
            xres = T(p2, "xres", [128, 8, NCOL], F32)
            R_x = [Res("x%d" % i) for i in range(18)]

            def xcols(ti):
                if ti == 0:
                    return 0, HALO
                if ti == 17:
                    return C_SMP, NSAMP
                return C_OWN + (ti - 1) * 128, 128
            nm2 = (T(p2, "n2_sq", [128, 8, 512], BF16), T(p2, "n2_rstd", [128, 512], F32), T(p2, "n2_tmp", [128, 8, 512], F32),
                   Res("n2_sq"), Res("n2_rstd"), Res("n2_tmp"))
            with ExitStack() as p2a:
                wo_bf = T(p2a, "wo_bf", [128, 8, D], BF16)
                R_wo = Res("wo")
                for kc in range(8):
                    dma("gpsimd", lambda e, kc=kc: e.dma_start(out=wo_bf[:, kc, :], in_=w_out[kc * 128:(kc + 1) * 128, :]), writes=[R_wo])
                xt2 = [T(p2a, "xt2_%d" % i, [128, D], F32) for i in range(2)]
                R_xt2 = [Res("xt2_0"), Res("xt2_1")]
                ytmp = T(p2a, "ytmp", [128, 4, 64], F32)
                R_yt = Res("ytmp")
                for ti in range(18):
                    c0, NT = xcols(ti)
                    xb, rxb = xt2[ti % 2], R_xt2[ti % 2]
                    src = x_ctx[2016:2048, :] if ti == 0 else (x_s if ti == 17 else x_own[(ti - 1) * 128:ti * 128, :])
                    dma("sync", lambda e, xb=xb, src=src, NT=NT: e.dma_start(out=xb[0:NT, :], in_=src), writes=[rxb])
                    for half in range(2):
                        pb, rb = gbank()
                        for cc in range(4):
                            c = half * 4 + cc
                            op("tensor", lambda e, pb=pb, cc=cc, c=c, xb=xb, NT=NT: e.transpose(out=pb[:, cc * 128:cc * 128 + NT], in_=xb[0:NT, c * 128:(c + 1) * 128],
                                                                                             identity=ident[0:NT, 0:NT]), reads=[rxb, R_const], writes=[rb])
                        op("vector", lambda e, pb=pb, half=half, c0=c0, NT=NT: e.tensor_copy(out=xres[:, half * 4:half * 4 + 4, c0:c0 + NT],
                                                                                           in_=pb[:, :].rearrange("p (c q) -> p c q", c=4)[:, :, 0:NT]), reads=[rb], writes=[R_x[ti]])
                    for half in range(2):
                        pb, rb = gbank()
                        for cc in range(4):
                            co = half * 4 + cc
                            for kc in range(8):
                                op("tensor", lambda e, pb=pb, cc=cc, co=co, kc=kc, c0=c0, NT=NT: e.matmul(pb[:, cc * 128:cc * 128 + NT], lhsT=wo_bf[:, kc, co * 128:(co + 1) * 128],
                                                                                                       rhs=oT_all[:, kc, c0:c0 + NT], start=(kc == 0), stop=(kc == 7)),
                                   reads=[R_wo, R_oT], writes=[rb])
                        for cc in range(4):
                            co = half * 4 + cc
                            xs = xres[:, co, c0:c0 + NT]
                            if ti != 17:
                                op("vector", lambda e, pb=pb, cc=cc, co=co, xs=xs, NT=NT: e.scalar_tensor_tensor(out=xs, in0=pb[:, cc * 128:cc * 128 + NT], scalar=mod_gate(0, 0)[:, co, 0:1],
                                                                                                              in1=xs, op0=ALU.mult, op1=ALU.add), reads=[rb, R_mod, R_x[ti]], writes=[R_x[ti]])
                            else:
                                g4 = fv(mod_gate(0, 0)[:, co, 1:17], [[1, 16], [0, 4]])
                                op("vector", lambda e, pb=pb, cc=cc, g4=g4: e.tensor_tensor(out=ytmp[:, cc, :].rearrange("p (s q) -> p s q", q=4),
                                                                                           in0=pb[:, cc * 128:cc * 128 + 64].rearrange("p (s q) -> p s q", q=4), in1=g4, op=ALU.mult),
                                   reads=[rb, R_mod], writes=[R_yt])
                                op("vector", lambda e, cc=cc, xs=xs: e.tensor_tensor(out=xs, in0=xs, in1=ytmp[:, cc, :], op=ALU.add), reads=[R_yt, R_x[ti]], writes=[R_x[ti]])
                S.barrier()

            def res_of_cols(c0, c1):
                out = []
                for ti in range(18):
                    a, n = xcols(ti)
                    if a < c1 and a + n > c0:
                        out.append(R_x[ti])
                return out

            def ffn(l):
                with ExitStack() as pf:
                    hTf = T(pf, "hTf_%d" % l, [128, 8, 512], BF16)
                    aT = oT_all[:].rearrange("p c n -> p (c n)")[:, 0:22 * 512].rearrange("p (m n) -> p m n", n=512)
                    sg = T(pf, "sg_%d" % l, [128, 512], F32)
                    wgb = [T(pf, "wgb%d_%d" % (i, l), [128, 8, 512], BF16) for i in range(2)]
                    wub = [T(pf, "wub%d_%d" % (i, l), [128, 8, 512], BF16) for i in range(2)]
                    wdb = [T(pf, "wdb%d_%d" % (i, l), [128, 22, 256], BF16) for i in range(2)]
                    ytmp = T(pf, "ytmpf_%d" % l, [128, 64], F32)
                    R_hTf, R_aT, R_sg, R_ytf = Res("hTf"), Res("aT"), Res("sg"), Res("ytf")
                    R_wg, R_wu, R_wd = [Res("wg0"), Res("wg1")], [Res("wu0"), Res("wu1")], [Res("wd0"), Res("wd1")]
                    passes = [(0, 512), (512, 1024), (1024, 1536), (1536, 2048), (2048, NCOL)]
                    kk = 0
                    kd = 0
                    for (c0, c1) in passes:
                        N = c1 - c0
                        rx = res_of_cols(c0, c1)
                        pc1 = min(c1, C_SMP)
                        if pc1 > c0:
                            norm_mod(nm2, xres[:, :, c0:pc1], rx[0], pc1 - c0, l, 1, False, hTf[:, :, 0:pc1 - c0], R_hTf)
                        if c1 > C_SMP:
                            norm_mod(nm2, xres[:, :, C_SMP:c1], R_x[17], 64, l, 1, True, hTf[:, :, C_SMP - c0:c1 - c0], R_hTf)
                        for mb in range(6):
                            mw = min(512, DFF - mb * 512)
                            wg_t, rg = wgb[kk % 2], R_wg[kk % 2]
                            wu_t, ru = wub[kk % 2], R_wu[kk % 2]
                            kk += 1
                            dma("gpsimd", lambda e, wg_t=wg_t, mb=mb, mw=mw: e.dma_start(out=wg_t[:, :, 0:mw], in_=ffn_wg[l, :, mb * 512:mb * 512 + mw].rearrange("(kc p) n -> p kc n", p=128)), writes=[rg])
                            dma("gpsimd", lambda e, wu_t=wu_t, mb=mb, mw=mw: e.dma_start(out=wu_t[:, :, 0:mw], in_=ffn_wu[l, :, mb * 512:mb * 512 + mw].rearrange("(kc p) n -> p kc n", p=128)), writes=[ru])
                            for mm in range(mw // 128):
                                m = mb * 4 + mm
                                pg_, rpg = PS[0 + (m % 2)], RPS[0 + (m % 2)]
                                pu_, rpu = PS[2 + (m % 2)], RPS[2 + (m % 2)]
                                for kc in range(8):
                                    op("tensor", lambda e, pg_=pg_, wg_t=wg_t, mm=mm, kc=kc, N=N: e.matmul(pg_[:, 0:N], lhsT=wg_t[:, kc, mm * 128:(mm + 1) * 128], rhs=hTf[:, kc, 0:N],
                                                                                                          start=(kc == 0), stop=(kc == 7)), reads=[rg, R_hTf], writes=[rpg])
                                for kc in range(8):
                                    op("tensor", lambda e, pu_=pu_, wu_t=wu_t, mm=mm, kc=kc, N=N: e.matmul(pu_[:, 0:N], lhsT=wu_t[:, kc, mm * 128:(mm + 1) * 128], rhs=hTf[:, kc, 0:N],
                                                                                                          start=(kc == 0), stop=(kc == 7)), reads=[ru, R_hTf], writes=[rpu])
                                op("scalar", lambda e, pg_=pg_, N=N: e.activation(out=sg[:, 0:N], in_=pg_[:, 0:N], func=AF.Silu), reads=[rpg], writes=[R_sg])
                                op("vector", lambda e, pu_=pu_, m=m, N=N: e.tensor_tensor(out=aT[:, m, 0:N], in0=sg[:, 0:N], in1=pu_[:, 0:N], op=ALU.mult), reads=[R_sg, rpu], writes=[R_aT])
                        for cb in range(4):
                            wd_t, rd = wdb[kd % 2], R_wd[kd % 2]
                            kd += 1
                            dma("gpsimd", lambda e, wd_t=wd_t, cb=cb: e.dma_start(out=wd_t[:], in_=ffn_wd[l, :, cb * 256:(cb + 1) * 256].rearrange("(m p) n -> p m n", p=128)), writes=[rd])
                            for c2 in range(2):
                                co = cb * 2 + c2
                                pb, rb = gbank()
                                for m in range(22):
                                    op("tensor", lambda e, pb=pb, wd_t=wd_t, c2=c2, m=m, N=N: e.matmul(pb[:, 0:N], lhsT=wd_t[:, m, c2 * 128:(c2 + 1) * 128], rhs=aT[:, m, 0:N],
                                                                                                      start=(m == 0), stop=(m == 21)), reads=[rd, R_aT], writes=[rb])
                                if pc1 > c0:
                                    n1 = pc1 - c0
                                    xs = xres[:, co, c0:pc1]
                                    op("vector", lambda e, pb=pb, co=co, xs=xs, n1=n1: e.scalar_tensor_tensor(out=xs, in0=pb[:, 0:n1], scalar=mod_gate(l, 1)[:, co, 0:1], in1=xs,
                                                                                                          op0=ALU.mult, op1=ALU.add), reads=[rb, R_mod] + rx, writes=rx[:-1] if c1 > C_SMP else rx)
                                if c1 > C_SMP:
                                    o0 = C_SMP - c0
                                    g4 = fv(mod_gate(l, 1)[:, co, 1:17], [[1, 16], [0, 4]])
                                    xs = xres[:, co, C_SMP:c1]
                                    op("vector", lambda e, pb=pb, g4=g4, o0=o0: e.tensor_tensor(out=ytmp[:, :].rearrange("p (s q) -> p s q", q=4),
                                                                                               in0=pb[:, o0:o0 + 64].rearrange("p (s q) -> p s q", q=4), in1=g4, op=ALU.mult),
                                       reads=[rb, R_mod], writes=[R_ytf])
                                    op("vector", lambda e, xs=xs: e.tensor_tensor(out=xs, in0=xs, in1=ytmp[:, :], op=ALU.add), reads=[R_ytf, R_x[17]], writes=[R_x[17]])
                    S.barrier()

            ffn(0)

            if stop_after >= 4:
              with ExitStack() as p4:
                oT_flat = oT_all[:].rearrange("p c n -> p (c n)")
                u2 = oT_flat[:, 0:4 * UEXT].bitcast(F32).rearrange("p (c n) -> p c n", c=2)
                pld = oT_flat[:, 4 * UEXT:6 * UEXT].rearrange("p (c n) -> p c n", c=2)
                R_u = Res("uT")
                rstd_all = T(p4, "rstd_all", [128, NCOL], F32)
                sA_ = T(p4, "poolA", [128, UEXT], F32)
                sB_ = T(p4, "poolB", [128, UEXT], F32)
                rcb = T(p4, "rcb", [128, UEXT], F32)
                pw_bf = T(p4, "pw_bf", [128, 4, 2, 256], BF16)
                hfl = T(p4, "hfl", [128, 1], F32)
                sp_sb = T(p4, "sp_sb", [128, 2, D], F32)
                osb = T(p4, "osb", [64, D], F32)
                osb2 = T(p4, "osb2", [64, D], F32)
                ytmp = T(p4, "ytmp4", [128, 64], F32)
                R_sA, R_sB, R_rc, R_pld, R_pw, R_hfl, R_sp, R_osb, R_yt4, R_rsa, R_osb2 = (
                    Res("sA"), Res("sB"), Res("rc"), Res("pld"), Res("pw"), Res("hfl"), Res("sp"), Res("osb"), Res("yt4"), Res("rsa"), Res("osb2"))
                dma("sync", lambda e: e.dma_start(out=hfl[:], in_=k_vt[0, 32:33].partition_broadcast(128)), writes=[R_hfl])
                op("vector", lambda e: e.memset(sA_[:, 0:16], 0.0), writes=[R_sA])
                op("vector", lambda e: e.memset(sB_[:, 0:16], 0.0), writes=[R_sB])
                for gi in range(4):
                    dma("gpsimd", lambda e, gi=gi: e.dma_start(out=pw_bf[:, gi, :, :], in_=pool_w[gi].rearrange("(k p) n -> p k n", p=128)), writes=[R_pw])
                for hb in range(2):
                    dma("sync", lambda e, hb=hb: e.dma_start(out=sp_sb[0:120, hb, :], in_=st_pool[hb * 8:(hb + 1) * 8].rearrange("s k d -> (s k) d")), writes=[R_sp])
                sq, rstd, tmp, R_sq, R_rstd, R_tmp = nm2
                spans = [(0, HALO)] + [(C_OWN + i * 512, C_OWN + (i + 1) * 512) for i in range(4)]
                for (c0, c1) in spans:
                    N = c1 - c0
                    rx = res_of_cols(c0, c1)
                    op("scalar", lambda e, c0=c0, c1=c1, N=N: e.activation(out=sq[:, :, 0:N], in_=xres[:, :, c0:c1], func=AF.Square), reads=rx, writes=[R_sq])
                    pb, rb = gbank()
                    for c in range(8):
                        op("tensor", lambda e, pb=pb, c=c, N=N: e.matmul(pb[:, 0:N], lhsT=ones_bf[:], rhs=sq[:, c, 0:N], start=(c == 0), stop=(c == 7)),
                           reads=[R_sq, R_const], writes=[rb])
                    op("scalar", lambda e, pb=pb, c0=c0, c1=c1, N=N: e.activation(out=rstd_all[:, c0:c1], in_=pb[:, 0:N], func=AF.Sqrt, bias=epsb[:, 0:1], scale=1.0 / D),
                       reads=[rb, R_const], writes=[R_rsa])
                    op("vector", lambda e, c0=c0, c1=c1: e.reciprocal(out=rstd_all[:, c0:c1], in_=rstd_all[:, c0:c1]), reads=[R_rsa], writes=[R_rsa])
                snew = nm2[2][:, :, 64:128]
                norm_mod(nm2, xres[:, :, C_SMP:NCOL], R_x[17], 64, 1, 0, True, snew, nm2[5])
                for c in range(8):
                    def dps(ps, rb, c=c):
                        op("vector", lambda e: e.tensor_copy(out=osb2[:, c * 128:(c + 1) * 128], in_=ps), reads=[rb], writes=[R_osb2])
                    transpose_to(dps, nm2[2][:, c, 64:128], nm2[5], 128, 64, None)
                for s_ in range(16):
                    dma("sync", lambda e, s_=s_: e.dma_start(out=pool_s[s_, 11:15, :], in_=osb2[4 * s_:4 * s_ + 4, :]), reads=[R_osb2])
                R_cp2 = Res("dcopy2")
                dma("sync", lambda e: e.dma_start(out=pool_s[:, 0:11, :], in_=st_pool[:, 4:15, :]), writes=[R_cp2])
                rx_p = [R_x[ti] for ti in range(17)]
                mspans = [(C_OWN + i * 512, 512, False) for i in range(4)] + [(C_SMP, 64, True)]
                for gi in range(4):
                    steps = gi + 1
                    dma("sync", lambda e, gi=gi: e.dma_start(out=rcb[:], in_=k_rc[gi].partition_broadcast(128)), writes=[R_rc])
                    for ci in range(2):
                        c = 2 * gi + ci
                        uc = u2[:, ci, :]
                        op("vector", lambda e, uc=uc, c=c: e.tensor_tensor(out=uc[:, 0:C_SMP], in0=xres[:, c, 0:C_SMP], in1=rstd_all[:, 0:C_SMP], op=ALU.mult),
                           reads=rx_p + [R_rsa], writes=[R_u])
                        op("scalar", lambda e, uc=uc, c=c: e.activation(out=uc[:, 0:C_SMP], in_=uc[:, 0:C_SMP], func=AF.Identity,
                                                                      bias=mod_sh(1, 0)[:, c, 0:1], scale=Amod[1][0][:, c, 0:1]), reads=[R_u, R_mod], writes=[R_u])
                        op("vector", lambda e, uc=uc: e.tensor_scalar(out=uc[:, 0:HALO], in0=uc[:, 0:HALO], scalar1=hfl[:, 0:1], scalar2=None, op0=ALU.mult),
                           reads=[R_u, R_hfl], writes=[R_u])
                        usm = uc[:, C_SMP:C_SMP + 304].rearrange("p (s k) -> p s k", k=19)
                        op("vector", lambda e, usm=usm, c=c: e.tensor_copy(out=usm[:, :, 15:19], in_=nm2[2][:, c, 64:128].rearrange("p (s q) -> p s q", q=4)),
                           reads=[nm2[5]], writes=[R_u])
                        for hb in range(2):
                            def dsp(ps, rb, hb=hb, usm=usm):
                                op("vector", lambda e: e.tensor_copy(out=usm[:, hb * 8:(hb + 1) * 8, 0:15], in_=ps.rearrange("p (s k) -> p s k", k=15)), reads=[rb], writes=[R_u])
                            transpose_to(dsp, sp_sb[0:120, hb, c * 128:(c + 1) * 128], R_sp, 120, 128, None)

                        def dpo(ps, rb, c=c):
                            op("vector", lambda e: e.tensor_copy(out=osb[0:15, c * 128:(c + 1) * 128], in_=ps), reads=[rb], writes=[R_osb])
                        transpose_to(dpo, uc[:, C_SMP - 15:C_SMP], R_u, 128, 15, None)
                        cur, rcur = uc, R_u
                        bufs = [(sA_, R_sA), (sB_, R_sB)]
                        for sidx in range(steps):
                            sh = 1 << sidx
                            nb, rnb = bufs[sidx % 2]
                            op("vector", lambda e, nb=nb, cur=cur, sh=sh: e.tensor_tensor(out=nb[:, 16:UEXT], in0=cur[:, 16:UEXT], in1=cur[:, 16 - sh:UEXT - sh], op=ALU.add),
                               reads=[rcur], writes=[rnb])
                            cur, rcur = nb, rnb
                        op("vector", lambda e, cur=cur: e.tensor_tensor(out=cur[:, 16:UEXT], in0=cur[:, 16:UEXT], in1=rcb[:, 16:UEXT], op=ALU.mult), reads=[rcur, R_rc], writes=[rcur])
                        op("vector", lambda e, cur=cur, ci=ci, uc=uc: e.tensor_tensor(out=pld[:, ci, 16:UEXT], in0=cur[:, 16:UEXT], in1=uc[:, 16:UEXT], op=ALU.subtract),
                           reads=[rcur, R_u], writes=[R_pld])
                    psm = pld[:, :, C_SMP:C_SMP + 304].rearrange("p c (s k) -> p c s k", k=19)
                    for (c0, N, smp) in mspans:
                        rx = [R_x[17]] if smp else res_of_cols(c0, c0 + N)
                        for co in (2 * gi, 2 * gi + 1):
                            pb, rb = gbank()
                            for ci in range(2):
                                rhs = psm[:, ci, :, 15:19] if smp else pld[:, ci, c0:c0 + N]
                                op("tensor", lambda e, pb=pb, gi=gi, ci=ci, co=co, rhs=rhs, N=N: e.matmul(pb[:, 0:N], lhsT=pw_bf[:, gi, ci, (co % 2) * 128:(co % 2 + 1) * 128], rhs=rhs,
                                                                                                     start=(ci == 0), stop=(ci == 1)), reads=[R_pw, R_pld], writes=[rb])
                            if not smp:
                                xs = xres[:, co, c0:c0 + N]
                                op("vector", lambda e, pb=pb, co=co, N=N: e.tensor_scalar(out=sA_[:, 0:N], in0=pb[:, 0:N], scalar1=vecT[:, co, 5:6], scalar2=mod_gate(1, 0)[:, co, 0:1],
                                                                                      op0=ALU.mult, op1=ALU.mult), reads=[rb, R_mod], writes=[R_sA])
                                op("vector", lambda e, xs=xs, N=N: e.tensor_tensor(out=xs, in0=xs, in1=sA_[:, 0:N], op=ALU.add), reads=[R_sA] + rx, writes=rx)
                            else:
                                g4 = fv(mod_gate(1, 0)[:, co, 1:17], [[1, 16], [0, 4]])
                                xs = xres[:, co, C_SMP:NCOL]
                                op("vector", lambda e, pb=pb, co=co: e.tensor_scalar(out=ytmp[:, :], in0=pb[:, 0:64], scalar1=vecT[:, co, 5:6], scalar2=None, op0=ALU.mult),
                                   reads=[rb, R_mod], writes=[R_yt4])
                                op("vector", lambda e, g4=g4: e.tensor_tensor(out=ytmp[:, :].rearrange("p (s q) -> p s q", q=4), in0=ytmp[:, :].rearrange("p (s q) -> p s q", q=4), in1=g4, op=ALU.mult),
                                   reads=[R_yt4, R_mod], writes=[R_yt4])
                                op("vector", lambda e, xs=xs: e.tensor_tensor(out=xs, in0=xs, in1=ytmp[:, :], op=ALU.add), reads=[R_yt4, R_x[17]], writes=[R_x[17]])
                dma("sync", lambda e: e.dma_start(out=pool_p, in_=osb[0:15, :]), reads=[R_osb])
                S.barrier()

            if stop_after >= 5:
                ffn(1)

            if stop_after >= 6:
              with ExitStack() as p6:
                yT = T(p6, "yT", [128, 8, 512], F32)
                R_yT = Res("yT")
                ob = [T(p6, "ob%d" % i, [128, D], F32) for i in range(2)]
                R_ob = [Res("ob0"), Res("ob1")]
                sq, rstd, tmp, R_sq, R_rstd, R_tmp = nm2
                spans = [(C_OWN + i * 512, 512) for i in range(4)] + [(C_SMP, 64)]
                ko = 0
                for (c0, N) in spans:
                    rx = res_of_cols(c0, c0 + N)
                    xap = xres[:, :, c0:c0 + N]
                    op("scalar", lambda e, xap=xap, N=N: e.activation(out=sq[:, :, 0:N], in_=xap, func=AF.Square), reads=rx, writes=[R_sq])
                    pb, rb = gbank()
                    for c in range(8):
                        op("tensor", lambda e, pb=pb, c=c, N=N: e.matmul(pb[:, 0:N], lhsT=ones_bf[:], rhs=sq[:, c, 0:N], start=(c == 0), stop=(c == 7)), reads=[R_sq, R_const], writes=[rb])
                    op("scalar", lambda e, pb=pb, N=N: e.activation(out=rstd[:, 0:N], in_=pb[:, 0:N], func=AF.Sqrt, bias=epsb[:, 0:1], scale=1.0 / D), reads=[rb, R_const], writes=[R_rstd])
                    op("vector", lambda e, N=N: e.reciprocal(out=rstd[:, 0:N], in_=rstd[:, 0:N]), reads=[R_rstd], writes=[R_rstd])
                    for c in range(8):
                        op("vector", lambda e, c=c, c0=c0, N=N: e.scalar_tensor_tensor(out=yT[:, c, 0:N], in0=xres[:, c, c0:c0 + N], scalar=vecT[:, c, 4:5], in1=rstd[:, 0:N],
                                                                                      op0=ALU.mult, op1=ALU.mult), reads=rx + [R_rstd, R_mod], writes=[R_yT])
                    nt = (N + 127) // 128
                    for tt in range(nt):
                        NT = min(128, N - tt * 128)
                        o_t, r_o = ob[ko % 2], R_ob[ko % 2]
                        ko += 1
                        for half in range(2):
                            pb, rb = gbank()
                            for cc in range(4):
                                c = half * 4 + cc
                                op("tensor", lambda e, pb=pb, cc=cc, c=c, tt=tt, NT=NT: e.transpose(out=pb[0:NT, cc * 128:(cc + 1) * 128], in_=yT[:, c, tt * 128:tt * 128 + NT], identity=ident[:]),
                                   reads=[R_yT, R_const], writes=[rb])
                            op("scalar", lambda e, pb=pb, half=half, o_t=o_t, NT=NT: e.copy(out=o_t[0:NT, half * 512:(half + 1) * 512], in_=pb[0:NT, :]), reads=[rb], writes=[r_o])
                        if c0 == C_SMP:
                            dma("sync", lambda e, o_t=o_t: e.dma_start(out=y_s, in_=o_t[0:64, :]), reads=[r_o])
                        else:
                            r0 = c0 - C_OWN + tt * 128
                            dma("sync", lambda e, o_t=o_t, r0=r0: e.dma_start(out=y_own[r0:r0 + 128, :], in_=o_t[:, :]), reads=[r_o])
                S.barrier()
        S.barrier()
        with nc.Block() as block:
            S.emit(block)
    return nc


def _host_consts(half):
    k = {}
    k["k_ident"] = np.eye(128, dtype=np.float32)
    k["k_anti"] = np.eye(128, dtype=np.float32)[::-1].copy()
    em = np.zeros((64, 4096), np.float32)
    em[np.arange(4096) // 64, np.arange(4096)] = 1.0
    k["k_emat"] = em
    kk = np.arange(128)[:, None]
    qq = np.arange(128)[None, :]
    k["k_ac"] = np.where(qq < kk, 0.0, NEGM).astype(np.float32)
    ohv = np.zeros((33, 384), np.float32)
    for i in range(383):
        d = i - 127
        if d < 0:
            ohv[32, i] = 1
        else:
            ohv[int(rel_bucket_np(d)), i] = 1
    k["k_ohv"] = ohv
    ohg = np.zeros((33, 8, 128), np.float32)
    for c in range(8):
        for qi in range(128):
            d = qi + 97 - 32 * c
            if d < 0:
                ohg[32, c, qi] = 1
            else:
                ohg[int(rel_bucket_np(d)), c, qi] = 1
    k["k_ohg"] = ohg.reshape(33, 1024)
    sel = np.zeros((3, 128, 18, 64), np.float32)
    sel[1] = -1.0
    j = np.arange(64)

    def fill(idx, rows, qa, ja, exists):
        for r in range(rows):
            q = qa[r]
            qblk = q // 64
            ex = exists & (q >= 0)
            forced = ex & ((ja == 0) | (ja == qblk) | (ja == qblk - 1))
            elig = ex & (ja <= qblk)
            sel[0, r, idx] = (elig & ~forced).astype(np.float32)
            sel[1, r, idx] = np.where(forced, 1e9, np.where(elig, 0.0, -1.0))
            sel[2, r, idx] = elig.astype(np.float32)
    ja_p = j - 32 * (1 - half)
    for t in range(16):
        qa = 2048 + 128 * t + np.arange(128) - 2048 * (1 - half)
        fill(t, 128, qa, ja_p, ja_p >= 0)
    qa = 1920 + np.arange(128) - 2048 * (1 - half)
    fill(16, 128, qa, ja_p, ja_p >= 0)
    fill(17, 4, 2048 + np.arange(4), j, j <= 32)
    k["k_sel"] = sel.reshape(3, 128, 18 * 64)
    vt = np.zeros((1, 40), np.float32)
    vt[0, :32] = 1.0
    if half == 0:
        vt[0, :16] = 0.0
    vt[0, 32] = float(half)
    k["k_vt"] = vt
    vc = np.ones((2, 128), np.float32)
    if half == 0:
        vc[0, :64] = 0.0
    vc[1, 64:] = 0.0
    k["k_vc"] = vc
    vcc = np.ones((128, 4), np.float32)
    vcc[:, 0] = vc[0]
    vcc[:, 1] = vc[1]
    vcc[:, 2] = (np.arange(128) < 4).astype(np.float32)
    k["k_vcc"] = vcc
    rc = np.zeros((4, UEXT), np.float32)
    for gi, w in enumerate((2, 4, 8, 16)):
        rc[gi, :] = 1.0 / w
        pos = half * 2048 + np.arange(2048)
        rc[gi, C_OWN:C_OWN + 2048] = 1.0 / np.minimum(pos + 1, w)
    k["k_rc"] = rc
    return k


_NC_CACHE = {}


def kernel(x_prompt, x_sample, c_prompt, c_sample, cache_k_cmp, cache_v_cmp, cache_k_sel, cache_v_sel,
           page_table, state_k_win, state_v_win, state_pool, rel_bias, ada_w, ada_b, norm_g, final_g,
           nsa_w_in, nsa_w_out, cmp_wk, cmp_wv, cmp_pe_k, cmp_pe_v, pool_w, pool_scale,
           ffn_wg, ffn_wu, ffn_wd, _stop_after=99):
    f = lambda a: np.ascontiguousarray(np.asarray(a, dtype=np.float32))
    x_prompt, x_sample = f(x_prompt), f(x_sample)
    if "nc" not in _NC_CACHE or _NC_CACHE.get("stop") != _stop_after:
        _NC_CACHE["nc"] = build_nc(_stop_after)
        _NC_CACHE["stop"] = _stop_after
    nc = _NC_CACHE["nc"]
    vecs = np.concatenate([f(norm_g).reshape(4, D), f(final_g).reshape(1, D), f(pool_scale).reshape(1, D), f(ada_b).reshape(12, D)], axis=0)
    shared = dict(
        rel_bias=f(rel_bias), ada_w=f(ada_w), w_in=f(nsa_w_in)[0], w_out=f(nsa_w_out)[0],
        cmp_w=np.stack([f(cmp_wk)[0], f(cmp_wv)[0]]), cmp_pe=np.stack([f(cmp_pe_k)[0], f(cmp_pe_v)[0]]),
        pool_w=f(pool_w)[0], ffn_wg=f(ffn_wg), ffn_wu=f(ffn_wu), ffn_wd=f(ffn_wd), vecs=vecs,
        cache0=f(cache_k_cmp).reshape(-1, 256), cache1=f(cache_v_cmp).reshape(-1, 256),
        cache2=f(cache_k_sel).reshape(-1, 256), cache3=f(cache_v_sel).reshape(-1, 256))
    if _stop_after < 2:
        for i in range(4):
            shared["cache%d" % i] = shared["cache%d" % i][0:128]
    hc = [_host_consts(0), _host_consts(1)]
    zeros_ctx = np.zeros((2048, D), np.float32)
    in_maps = []
    pt = np.asarray(page_table, dtype=np.int32)
    for c in range(8):
        b, half = c // 2, c % 2
        s0 = c * 16
        m = dict(shared)
        m.update(hc[half])
        m["x_ctx"] = x_prompt[b, 0:2048] if half == 1 else zeros_ctx
        m["x_own"] = x_prompt[b, half * 2048:(half + 1) * 2048]
        m["x_s"] = x_sample[s0:s0 + 16].reshape(64, D)
        m["c_all"] = np.concatenate([f(c_prompt)[b:b + 1], f(c_sample)[s0:s0 + 16]], axis=0)
        m["ptab"] = np.ascontiguousarray(pt[s0:s0 + 16].reshape(1, 256))
        m["st_kw"] = f(state_k_win)[0, s0:s0 + 16].reshape(16, 512, 256)
        m["st_vw"] = f(state_v_win)[0, s0:s0 + 16].reshape(16, 512, 256)
        m["st_pool"] = f(state_pool)[0, s0:s0 + 16]
        in_maps.append(m)
    res = run_bass_kernel_spmd(nc, in_maps, core_ids=list(range(8)))
    R = res.results
    y_prompt = np.zeros((4, 4096, D), np.float32)
    y_sample = np.zeros((128, 4, D), np.float32)
    kvp = [np.zeros((1, 4, 4096, 4, 64), np.float32) for _ in range(4)]
    kwp = [np.zeros((1, 4, 512, 4, 64), np.float32) for _ in range(2)]
    poolp = np.zeros((1, 4, 15, D), np.float32)
    kvs = [np.zeros((1, 128, 4, 4, 64), np.float32) for _ in range(4)]
    kws = [np.zeros((1, 128, 512, 4, 64), np.float32) for _ in range(2)]
    pools = np.zeros((1, 128, 15, D), np.float32)
    for c in range(8):
        b, half = c // 2, c % 2
        s0 = c * 16
        r = R[c]
        y_prompt[b, half * 2048:(half + 1) * 2048] = r["y_own"]
        y_sample[s0:s0 + 16] = r["y_s"].reshape(16, 4, D)
        for t in range(4):
            kvp[t][0, b, half * 2048:(half + 1) * 2048] = r["kvraw_p"][t].reshape(2048, 4, 64)
            kvs[t][0, s0:s0 + 16] = r["kvraw_s"][t].reshape(16, 4, 4, 64)
        for t in range(2):
            kws[t][0, s0:s0 + 16] = r["kvwin_s"][t].reshape(16, 512, 4, 64)
        pools[0, s0:s0 + 16] = r["pool_s"]
        if half == 1:
            for t in range(2):
                kwp[t][0, b] = r["kvwin_p"][t].reshape(512, 4, 64)
            poolp[0, b] = r["pool_p"]
    return (y_prompt, y_sample, kvp[0], kvp[1], kvp[2], kvp[3], kwp[0], kwp[1], poolp,
            kvs[0], kvs[1], kvs[2], kvs[3], kws[0], kws[1], pools)
```
